# Optimizing a Trainium2 kernel written in Bass

```python
import math
import jax, jax.numpy as jnp
from jax import lax
import numpy as np

D_MODEL = 2048
BATCH = 16
SEQ = 256
DEPTH = 4
DEC_BATCH = 8
DEC_SEQ = 2048
PAST_LEN = 256

GRID_W = 64
N_EVEN = (DEPTH + 1) // 2
N_ODD = DEPTH // 2
EPS = 1e-6
CONV_W = 4
CONV_LEFT = CONV_W // 2

D_RNN = D_MODEL // 2
LRU_BLOCKS = 8
LRU_BW = D_RNN // LRU_BLOCKS
LRU_C = 8.0
DA_HEADS = 8
DA_HALF = 64
DA_VDIM = 2 * DA_HALF
D_ATTN = DA_HEADS * DA_VDIM
EVEN_SPLITS = (D_RNN, 2 * D_RNN, 2 * D_RNN + D_ATTN, 2 * D_RNN + 2 * D_ATTN)
EVEN_IN = 2 * D_RNN + 3 * D_ATTN
EVEN_MIX = D_RNN + D_ATTN
Q_BLOCK = 128
ROPE_THETA = 10000.0

D_INNER = 2 * D_MODEL
SSD_HEADDIM = 64
SSD_HEADS = D_INNER // SSD_HEADDIM
SSD_GROUPS = 8
SSD_HPG = SSD_HEADS // SSD_GROUPS
SSD_STATE = 128
SSD_CHUNK = 128
D_XBC = D_INNER + 2 * SSD_GROUPS * SSD_STATE
ODD_IN = D_INNER + D_XBC + 2 * SSD_HEADS

D_FF = -(-8 * D_MODEL // 768) * 256

kernel_name = "hybrid_diffusion_rglru_diffattn_ssd_step"


def rms_norm(x, g):
    xf = x.astype(jnp.float32)
    y = xf * lax.rsqrt(jnp.mean(xf * xf, axis=-1, keepdims=True) + EPS)
    return (y * g.astype(jnp.float32)).astype(x.dtype)


def ada_params(cond, w, b):
    m = jax.nn.silu(cond) @ w + b
    return m.reshape(cond.shape[0], 1, 6, D_MODEL)


def modulate(x, g, shift, scale):
    return rms_norm(x, g) * (1.0 + scale) + shift


def depthwise_conv(x, w, b):
    L = x.shape[1]
    xp = jnp.pad(x, ((0, 0), (CONV_LEFT, CONV_W - 1 - CONV_LEFT), (0, 0)))
    out = b
    for t in range(CONV_W):
        out = out + xp[:, t:t + L] * w[t]
    return out


def axial_rope_tables(L):
    rows = L // GRID_W
    row = jnp.repeat(jnp.arange(rows), GRID_W).astype(jnp.float32)
    col = jnp.tile(jnp.arange(GRID_W), rows).astype(jnp.float32)
    quarter = DA_HALF // 4
    inv = ROPE_THETA ** (-jnp.arange(quarter, dtype=jnp.float32) / quarter)
    ang_r = row[:, None] * inv
    ang_c = col[:, None] * inv
    ang = jnp.concatenate([ang_r, ang_r, ang_c, ang_c], axis=-1)
    return jnp.cos(ang), jnp.sin(ang)


def apply_axial_rope(x, cos, sin):
    q = DA_HALF // 4
    xf = x.astype(jnp.float32)
    x1, x2, x3, x4 = xf[..., :q], xf[..., q:2 * q], xf[..., 2 * q:3 * q], xf[..., 3 * q:]
    rot = jnp.concatenate([-x2, x1, -x4, x3], axis=-1)
    return (xf * cos[:, None, None, :] + rot * sin[:, None, None, :]).astype(x.dtype)


def _lin_combine(l, r):
    al, bl = l
    ar, br = r
    return al * ar, ar * bl + br


def rglru_scan_dir(xc, w_r, b_r, w_i, b_i, lam, h0):
    Bsz, L, _ = xc.shape
    xb = xc.reshape(Bsz, L, LRU_BLOCKS, LRU_BW)
    r = jax.nn.sigmoid(jnp.einsum('blki,kij->blkj', xb, w_r.astype(jnp.float32)).reshape(Bsz, L, D_RNN) + b_r)
    gi = jax.nn.sigmoid(jnp.einsum('blki,kij->blkj', xb, w_i.astype(jnp.float32)).reshape(Bsz, L, D_RNN) + b_i)
    log_a = -LRU_C * r * jax.nn.softplus(-lam.astype(jnp.float32))
    a = jnp.exp(log_a)
    bt = jnp.sqrt(-jnp.expm1(2.0 * log_a)) * (gi * xc)
    bt = bt.at[:, 0].add(a[:, 0] * h0)
    _, hs = lax.associative_scan(_lin_combine, (a, bt), axis=1)
    return hs


def rglru_bidir(xc, w_r, b_r, w_i, b_i, lam, h0):
    hf = rglru_scan_dir(xc, w_r[0], b_r[0], w_i[0], b_i[0], lam[0], h0[:, 0])
    hb = jnp.flip(rglru_scan_dir(jnp.flip(xc, 1), w_r[1], b_r[1], w_i[1], b_i[1], lam[1], h0[:, 1]), 1)
    return hf + hb, jnp.stack([hf[:, -1], hb[:, 0]], axis=1)


def diff_attention(q, k, v, lam):
    s = jnp.einsum('bqhmd,bkhmd->bhmqk', q, k).astype(jnp.float32) * (DA_HALF ** -0.5)
    p = jax.nn.softmax(s, axis=-1)
    w = p[:, :, 0] - lam * p[:, :, 1]
    return jnp.einsum('bhqk,bkhd->bqhd', w.astype(v.dtype), v)


def even_mixer(h, ctx, rope, lam_init, w_in, w_out, conv_w, conv_b, w_r, b_r, w_i, b_i, lru_lam,
               q_norm, k_norm, da_lam, subln):
    Bsz, L, _ = h.shape
    f32 = jnp.float32
    u = h @ w_in
    gate, xr, q, k, v = jnp.split(u, EVEN_SPLITS, axis=-1)
    xc = depthwise_conv(xr, conv_w, conv_b).astype(f32)
    h0 = jnp.zeros((Bsz, 2, D_RNN), f32) if ctx is None else ctx[2].astype(f32)
    rec, s_fin = rglru_bidir(xc, w_r, b_r, w_i, b_i, lru_lam, h0)
    rec = (rec * jax.nn.gelu(gate.astype(f32))).astype(h.dtype)
    q = rms_norm(q.reshape(Bsz, L, DA_HEADS, 2, DA_HALF), q_norm)
    k = rms_norm(k.reshape(Bsz, L, DA_HEADS, 2, DA_HALF), k_norm)
    v = v.reshape(Bsz, L, DA_HEADS, DA_VDIM)
    if ctx is None:
        k_all, v_all = k, v
    else:
        cos, sin = rope
        q = apply_axial_rope(q, cos, sin)
        k_lat = apply_axial_rope(k, cos, sin)
        k_ctx = ctx[0].reshape(Bsz, -1, DA_HEADS, 2, DA_HALF).astype(k.dtype)
        k_all = jnp.concatenate([k_lat, k_ctx], axis=1)
        v_all = jnp.concatenate([v, ctx[1].astype(v.dtype)], axis=1)
    dl = da_lam.astype(f32)
    lam = jnp.exp(jnp.sum(dl[0] * dl[1])) - jnp.exp(jnp.sum(dl[2] * dl[3])) + lam_init
    qb = q.reshape(Bsz, L // Q_BLOCK, Q_BLOCK, DA_HEADS, 2, DA_HALF).swapaxes(0, 1)
    att = lax.map(lambda qq: diff_attention(qq, k_all, v_all, lam), qb)
    att = att.swapaxes(0, 1).reshape(Bsz, L, DA_HEADS, DA_VDIM)
    att = (rms_norm(att, subln) * (1.0 - lam_init)).reshape(Bsz, L, D_ATTN).astype(h.dtype)
    out = jnp.concatenate([rec, att], axis=-1) @ w_out
    return out, (k.reshape(Bsz, L, DA_HEADS, 2 * DA_HALF), v, s_fin)


def segsum(a):
    T = a.shape[-1]
    cs = jnp.cumsum(a, axis=-1)
    diff = cs[..., :, None] - cs[..., None, :]
    return jnp.where(jnp.tril(jnp.ones((T, T), bool)), diff, -jnp.inf)


def ssd_chunked(X, A, Bm, Cm, h0):
    b, l, _, p = X.shape
    c, q = l // SSD_CHUNK, SSD_CHUNK
    g, e, n = SSD_GROUPS, SSD_HPG, SSD_STATE
    X = X.reshape(b, c, q, g, e, p)
    A = A.reshape(b, c, q, g, e).transpose(0, 3, 4, 1, 2)
    Bm = Bm.reshape(b, c, q, g, n)
    Cm = Cm.reshape(b, c, q, g, n)
    A_cs = jnp.cumsum(A, axis=-1)
    Lmat = jnp.exp(segsum(A))
    CB = jnp.einsum('bclgn,bcsgn->bgcls', Cm, Bm)
    Y_diag = jnp.einsum('bgcls,bgecls,bcsgep->bclgep', CB, Lmat, X)
    decay_states = jnp.exp(A_cs[..., -1:] - A_cs)
    states = jnp.einsum('bclgn,bgecl,bclgep->bcgepn', Bm, decay_states, X)
    states = jnp.concatenate([h0.reshape(b, 1, g, e, p, n), states], axis=1)
    chunk_tot = jnp.pad(A_cs[..., -1], ((0, 0), (0, 0), (0, 0), (1, 0)))
    decay_chunk = jnp.exp(segsum(chunk_tot))
    states = jnp.einsum('bgezc,bcgepn->bzgepn', decay_chunk, states)
    prev_states, final = states[:, :-1], states[:, -1]
    Y_off = jnp.einsum('bclgn,bcgepn,bgecl->bclgep', Cm, prev_states, jnp.exp(A_cs))
    Y = (Y_diag + Y_off).reshape(b, l, g * e, p)
    return Y, final.reshape(b, g * e, p, n)


def odd_mixer(h, h0, w_in, conv_w, conv_b, dt_bias, a_log, d_skip, norm_w, w_out):
    Bsz, L, _ = h.shape
    f32 = jnp.float32
    u = h @ w_in
    z, xbc, dt = jnp.split(u, (D_INNER, D_INNER + D_XBC), axis=-1)
    xbc = jax.nn.silu(depthwise_conv(xbc, conv_w, conv_b)).astype(f32)
    xs, bm, cm = jnp.split(xbc, (D_INNER, D_INNER + SSD_GROUPS * SSD_STATE), axis=-1)
    xs = xs.reshape(Bsz, L, SSD_HEADS, SSD_HEADDIM)
    bm = bm.reshape(Bsz, L, SSD_GROUPS, SSD_STATE)
    cm = cm.reshape(Bsz, L, SSD_GROUPS, SSD_STATE)
    dt = jax.nn.softplus(dt.astype(f32).reshape(Bsz, L, 2, SSD_HEADS) + dt_bias.astype(f32))
    a = -jnp.exp(a_log.astype(f32))
    if h0 is None:
        h0 = jnp.zeros((Bsz, 2, SSD_HEADS, SSD_HEADDIM, SSD_STATE), f32)
    h0 = h0.astype(f32)
    yf, sf = ssd_chunked(xs * dt[:, :, 0, :, None], a[0] * dt[:, :, 0], bm, cm, h0[:, 0])
    yb, sb = ssd_chunked(jnp.flip(xs * dt[:, :, 1, :, None], 1), jnp.flip(a[1] * dt[:, :, 1], 1),
                         jnp.flip(bm, 1), jnp.flip(cm, 1), h0[:, 1])
    y = yf + jnp.flip(yb, 1) + d_skip.astype(f32)[:, None] * xs
    y = y.reshape(Bsz, L, D_INNER) * jax.nn.silu(z.astype(f32))
    yg = y.reshape(Bsz, L, SSD_GROUPS, D_INNER // SSD_GROUPS)
    yg = yg * lax.rsqrt(jnp.mean(yg * yg, axis=-1, keepdims=True) + EPS)
    yg = yg.reshape(Bsz, L, D_INNER) * norm_w.astype(f32)
    return yg.astype(h.dtype) @ w_out, jnp.stack([sf, sb], axis=1)


def swiglu(h, w_in, w_out):
    g, u = jnp.split(h @ w_in, 2, axis=-1)
    return (jax.nn.silu(g) * u) @ w_out


def setup_inputs(seed: int = 0) -> dict:
    key = jax.random.key(seed)
    ks = jax.random.split(key, 40)
    f32 = jnp.float32

    def nrm(k, shape, scale):
        return jax.random.normal(k, shape, f32) * scale

    def gain(k, shape):
        return 1.0 + 0.02 * jax.random.normal(k, shape, f32)

    u_lru = jax.random.uniform(ks[13], (N_EVEN, 2, D_RNN), f32, 0.9, 0.999)
    a_base = u_lru ** (1.0 / LRU_C)
    lru_lambda = jnp.log(a_base) - jnp.log1p(-a_base)
    dt0 = jnp.exp(jax.random.uniform(ks[25], (N_ODD, 2, SSD_HEADS), f32, math.log(1e-3), math.log(1e-1)))
    ssd_dt_bias = dt0 + jnp.log(-jnp.expm1(-dt0))
    ssd_a_log = jnp.log(jax.random.uniform(ks[26], (N_ODD, 2, SSD_HEADS), f32, 1.0, 16.0))
    return {
        "x_prompt": nrm(ks[0], (BATCH, SEQ, D_MODEL), 1.0),
        "x_sample": nrm(ks[1], (DEC_BATCH, DEC_SEQ, D_MODEL), 1.0),
        "c": nrm(ks[2], (DEC_BATCH, D_MODEL), 1.0),
        "cache_attn_k": nrm(ks[3], (DEC_BATCH, N_EVEN, PAST_LEN, DA_HEADS, 2 * DA_HALF), 1.0),
        "cache_attn_v": nrm(ks[4], (DEC_BATCH, N_EVEN, PAST_LEN, DA_HEADS, DA_VDIM), 1.0),
        "state_lru": nrm(ks[5], (DEC_BATCH, N_EVEN, 2, D_RNN), 0.5),
        "state_ssd": nrm(ks[6], (DEC_BATCH, N_ODD, 2, SSD_HEADS, SSD_HEADDIM, SSD_STATE), 0.1),
        "c_ctx": nrm(ks[7], (D_MODEL,), 1.0),
        "w_ada": nrm(ks[8], (DEPTH, D_MODEL, 6 * D_MODEL), 0.5 * D_MODEL ** -0.5),
        "b_ada": nrm(ks[9], (DEPTH, 6 * D_MODEL), 0.02),
        "norm_g": gain(ks[10], (DEPTH, 2, D_MODEL)),
        "lru_conv_w": nrm(ks[11], (N_EVEN, CONV_W, D_RNN), CONV_W ** -0.5),
        "lru_conv_b": nrm(ks[12], (N_EVEN, D_RNN), 0.02),
        "lru_w_r": nrm(ks[14], (N_EVEN, 2, LRU_BLOCKS, LRU_BW, LRU_BW), LRU_BW ** -0.5),
        "lru_b_r": nrm(ks[15], (N_EVEN, 2, D_RNN), 0.02),
        "lru_w_i": nrm(ks[16], (N_EVEN, 2, LRU_BLOCKS, LRU_BW, LRU_BW), LRU_BW ** -0.5),
        "lru_b_i": nrm(ks[17], (N_EVEN, 2, D_RNN), 0.02),
        "lru_lambda": lru_lambda,
        "even_w_in": nrm(ks[18], (N_EVEN, D_MODEL, EVEN_IN), D_MODEL ** -0.5),
        "even_w_out": nrm(ks[19], (N_EVEN, EVEN_MIX, D_MODEL), EVEN_MIX ** -0.5),
        "da_q_norm": gain(ks[20], (N_EVEN, 2, DA_HALF)),
        "da_k_norm": gain(ks[21], (N_EVEN, 2, DA_HALF)),
        "da_lambda": nrm(ks[22], (N_EVEN, 4, DA_HALF), 0.1),
        "da_subln": gain(ks[23], (N_EVEN, DA_VDIM)),
        "ssd_w_in": nrm(ks[24], (N_ODD, D_MODEL, ODD_IN), D_MODEL ** -0.5),
        "ssd_conv_w": nrm(ks[27], (N_ODD, CONV_W, D_XBC), CONV_W ** -0.5),
        "ssd_conv_b": nrm(ks[28], (N_ODD, D_XBC), 0.02),
        "ssd_dt_bias": ssd_dt_bias,
        "ssd_a_log": ssd_a_log,
        "ssd_d": gain(ks[29], (N_ODD, SSD_HEADS)),
        "ssd_norm_w": gain(ks[30], (N_ODD, D_INNER)),
        "ssd_w_out": nrm(ks[31], (N_ODD, D_INNER, D_MODEL), D_INNER ** -0.5),
        "ffn_w_in": nrm(ks[32], (DEPTH, D_MODEL, 2 * D_FF), D_MODEL ** -0.5),
        "ffn_w_out": nrm(ks[33], (DEPTH, D_FF, D_MODEL), D_FF ** -0.5),
    }


def reference(x_prompt, x_sample, c, cache_attn_k, cache_attn_v, state_lru, state_ssd, c_ctx,
              w_ada, b_ada, norm_g, lru_conv_w, lru_conv_b, lru_w_r, lru_b_r, lru_w_i, lru_b_i,
              lru_lambda, even_w_in, even_w_out, da_q_norm, da_k_norm, da_lambda, da_subln,
              ssd_w_in, ssd_conv_w, ssd_conv_b, ssd_dt_bias, ssd_a_log, ssd_d, ssd_norm_w,
              ssd_w_out, ffn_w_in, ffn_w_out):
    rope = axial_rope_tables(x_sample.shape[1])
    yp, ys = x_prompt, x_sample
    new_k, new_v, new_lru, new_ssd = [], [], [], []
    for i in range(DEPTH):
        j = i // 2
        mp = ada_params(c_ctx[None, :], w_ada[i], b_ada[i])
        ms = ada_params(c, w_ada[i], b_ada[i])
        hp = modulate(yp, norm_g[i, 0], mp[:, :, 0], mp[:, :, 1])
        hs = modulate(ys, norm_g[i, 0], ms[:, :, 0], ms[:, :, 1])
        if i % 2 == 0:
            lam_init = 0.8 - 0.6 * math.exp(-0.3 * i)
            ew = (even_w_in[j], even_w_out[j], lru_conv_w[j], lru_conv_b[j], lru_w_r[j], lru_b_r[j],
                  lru_w_i[j], lru_b_i[j], lru_lambda[j], da_q_norm[j], da_k_norm[j], da_lambda[j],
                  da_subln[j])
            op, (kc, vc, sc) = even_mixer(hp, None, None, lam_init, *ew)
            ctx = (cache_attn_k[:, j], cache_attn_v[:, j], state_lru[:, j])
            os_, _ = even_mixer(hs, ctx, rope, lam_init, *ew)
            new_k.append(kc)
            new_v.append(vc)
            new_lru.append(sc)
        else:
            ow = (ssd_w_in[j], ssd_conv_w[j], ssd_conv_b[j], ssd_dt_bias[j], ssd_a_log[j], ssd_d[j],
                  ssd_norm_w[j], ssd_w_out[j])
            op, sc = odd_mixer(hp, None, *ow)
            os_, _ = odd_mixer(hs, state_ssd[:, j], *ow)
            new_ssd.append(sc)
        yp = yp + mp[:, :, 2] * op
        ys = ys + ms[:, :, 2] * os_
        hp = modulate(yp, norm_g[i, 1], mp[:, :, 3], mp[:, :, 4])
        hs = modulate(ys, norm_g[i, 1], ms[:, :, 3], ms[:, :, 4])
        yp = yp + mp[:, :, 5] * swiglu(hp, ffn_w_in[i], ffn_w_out[i])
        ys = ys + ms[:, :, 5] * swiglu(hs, ffn_w_in[i], ffn_w_out[i])
    return (yp, ys, jnp.stack(new_k, axis=1), jnp.stack(new_v, axis=1),
            jnp.stack(new_lru, axis=1), jnp.stack(new_ssd, axis=1))
```

```python
import math
from contextlib import ExitStack

import numpy as np
import concourse.bass as bass
import concourse.mybir as mybir
from concourse.bass_utils import run_bass_kernel_spmd

F32 = mybir.dt.float32
BF16 = mybir.dt.bfloat16
AF = mybir.ActivationFunctionType
ALU = mybir.AluOpType

D = 2048
KC = 16
DEPTH = 4
NCORES = 8
S_LEN = 2048
P_LEN = 256
T = S_LEN + 2 * P_LEN
NTILE = T // 512
NBLK = T // 128
EPS = 1e-6
D_RNN = 1024
D_FF = 5632
FFC = D_FF // 128
D_INNER = 4096
SEGS = [(0, S_LEN), (S_LEN, S_LEN + P_LEN), (S_LEN + P_LEN, T)]
GRID_W = 64

C_ID, C_ROT, C_MF, C_MB, C_SL, C_SU, C_ONE, C_B64 = range(8)
NCONSTF = 8


class Tok:
    __slots__ = ("w", "r")

    def __init__(self):
        self.w = None
        self.r = {}


class View:
    __slots__ = ("ap", "toks")

    def __init__(self, ap, toks):
        self.ap = ap
        self.toks = toks


class Tile:
    def __init__(self, t, ntok=1):
        self.t = t
        self.toks = [Tok() for _ in range(ntok)]

    def __getitem__(self, idx):
        return View(self.t[idx], self.toks)

    def v(self, ap, ti=None):
        if ti is None:
            toks = self.toks
        elif isinstance(ti, int):
            toks = [self.toks[ti]]
        else:
            toks = [self.toks[i] for i in ti]
        return View(ap, toks)


ENGS = ("pe", "act", "dve", "pool", "sp")
import os
SAME_ENGINE_SYNC = os.environ.get("NOSES") is None
PROFILE_SCOPES = False


class Prog:
    def __init__(self, nc, es):
        self.nc = nc
        self.es = es
        self.ops = {e: [] for e in ENGS}
        self.count = {e: 0 for e in ENGS}
        self.known = {e: {} for e in ENGS}
        self.esem = {e: es.enter_context(nc.semaphore("sem_" + e)) for e in ENGS}
        self.sems = dict(self.esem)
        self.dma_pool = {}
        self.dma_rr = {}
        self.dma_val = {}
        for q, n in (("sp", 40), ("pool", 24)):
            names = []
            for i in range(n):
                nm = "dq_%s_%d" % (q, i)
                self.sems[nm] = es.enter_context(nc.semaphore(nm))
                self.dma_val[nm] = 0
                names.append(nm)
            self.dma_pool[q] = names
            self.dma_rr[q] = 0
        self.uid = 0
        self.scope = "setup"

    def name(self, base):
        self.uid += 1
        return "%s_%d" % (base, self.uid)

    def sb(self, es, base, shape, dtype, ntok=1):
        return Tile(es.enter_context(self.nc.sbuf_tensor(self.name(base), list(shape), dtype)), ntok)

    def _deps(self, reads, writes):
        deps = {}

        def add(s, v):
            if deps.get(s, 0) < v:
                deps[s] = v

        for vw in reads:
            for t in vw.toks:
                if t.w is not None:
                    add(*t.w)
        for vw in writes:
            for t in vw.toks:
                if t.w is not None:
                    add(*t.w)
                for s, v in t.r.items():
                    add(s, v)
        return deps

    def _waits(self, eng, deps):
        waits = []
        kn = self.known[eng]
        for s, v in deps.items():
            if s == eng and (eng == "pe" or not SAME_ENGINE_SYNC):
                continue
            if kn.get(s, 0) >= v:
                continue
            kn[s] = v
            waits.append((s, v))
        return waits

    def op(self, eng, fn, reads=(), writes=()):
        deps = self._deps(reads, writes)
        waits = self._waits(eng, deps)
        self.count[eng] += 1
        k = self.count[eng]
        self.ops[eng].append((waits, fn, (eng, 1), self.scope))
        for vw in reads:
            for t in vw.toks:
                if t.r.get(eng, 0) < k:
                    t.r[eng] = k
        for vw in writes:
            for t in vw.toks:
                t.w = (eng, k)
                t.r = {}

    def dma(self, q, out, in_, **kw):
        pool = self.dma_pool[q]
        s = pool[self.dma_rr[q] % len(pool)]
        self.dma_rr[q] += 1
        deps = self._deps([in_], [out])
        prev = self.dma_val[s]
        if prev > 0 and deps.get(s, 0) < prev:
            deps[s] = prev
        waits = self._waits(q, deps)
        target = prev + 16
        self.dma_val[s] = target
        oap, iap = out.ap, in_.ap

        def fn(e):
            return e.dma_start(out=oap, in_=iap, **kw)

        self.ops[q].append((waits, fn, (s, 16), self.scope))
        for t in in_.toks:
            if t.r.get(s, 0) < target:
                t.r[s] = target
        for t in out.toks:
            t.w = (s, target)
            t.r = {}

    def barrier(self):
        for e in ENGS:
            deps = {}
            for e2 in ENGS:
                if e2 != e and self.count[e2] > 0:
                    deps[e2] = self.count[e2]
            if e != "pe" and self.count[e] > 0:
                deps[e] = self.count[e]
            for s, v in self.dma_val.items():
                if v > 0:
                    deps[s] = v
            waits = []
            kn = self.known[e]
            for s, v in deps.items():
                if kn.get(s, 0) >= v:
                    continue
                kn[s] = v
                waits.append((s, v))
            if waits:
                self.ops[e].append((waits, None, None, self.scope))

    def mm(self, out, lhsT, rhs, start=True, stop=True, sgc=False):
        o, l, r = out.ap, lhsT.ap, rhs.ap
        if sgc:
            self.op("pe", lambda e: e.matmul(o, l, r, start=start, stop=stop, skip_group_check=True), [lhsT, rhs], [out])
        else:
            self.op("pe", lambda e: e.matmul(o, l, r, start=start, stop=stop), [lhsT, rhs], [out])

    def transpose(self, out, in_, ident):
        o, i, d = out.ap, in_.ap, ident.ap
        self.op("pe", lambda e: e.transpose(out=o, in_=i, identity=d), [in_, ident], [out])

    def act(self, out, in_, func, bias=None, scale=None, accum_out=None, eng="act"):
        kw = {}
        reads = [in_]
        writes = [out]
        if bias is not None:
            if isinstance(bias, View):
                kw["bias"] = bias.ap
                reads.append(bias)
            else:
                kw["bias"] = float(bias)
        if scale is not None:
            if isinstance(scale, View):
                kw["scale"] = scale.ap
                reads.append(scale)
            else:
                kw["scale"] = float(scale)
        if accum_out is not None:
            kw["accum_out"] = accum_out.ap
            writes.append(accum_out)
        o, i = out.ap, in_.ap
        self.op("act", lambda e: e.activation(out=o, in_=i, func=func, **kw), reads, writes)

    def tt(self, eng, out, in0, in1, op):
        o, a, b = out.ap, in0.ap, in1.ap
        self.op(eng, lambda e: e.tensor_tensor(out=o, in0=a, in1=b, op=op), [in0, in1], [out])

    def ts(self, eng, out, in0, s1, s2, op0, op1=None):
        reads = [in0]
        a1 = s1
        a2 = s2
        if isinstance(s1, View):
            reads.append(s1)
            a1 = s1.ap
        if isinstance(s2, View):
            reads.append(s2)
            a2 = s2.ap
        o, a = out.ap, in0.ap
        if op1 is None:
            self.op(eng, lambda e: e.tensor_scalar(out=o, in0=a, scalar1=a1, scalar2=None, op0=op0), reads, [out])
        else:
            self.op(eng, lambda e: e.tensor_scalar(out=o, in0=a, scalar1=a1, scalar2=a2, op0=op0, op1=op1),
                    reads, [out])

    def stt(self, eng, out, in0, scalar, in1, op0, op1):
        reads = [in0, in1]
        sc = scalar
        if isinstance(scalar, View):
            reads.append(scalar)
            sc = scalar.ap
        o, a, b = out.ap, in0.ap, in1.ap
        self.op(eng, lambda e: e.scalar_tensor_tensor(out=o, in0=a, scalar=sc, in1=b, op0=op0, op1=op1),
                reads, [out])

    def copy(self, eng, out, in_):
        o, i = out.ap, in_.ap
        if eng == "act":
            self.op("act", lambda e: e.copy(out=o, in_=i), [in_], [out])
        else:
            self.op(eng, lambda e: e.tensor_copy(out=o, in_=i), [in_], [out])

    def recip(self, out, in_):
        o, i = out.ap, in_.ap
        self.op("dve", lambda e: e.reciprocal(out=o, in_=i), [in_], [out])

    def memset(self, eng, out, val):
        o = out.ap
        self.op(eng, lambda e: e.memset(o, val), [], [out])

    def scan(self, out, d0, d1, initial):
        reads = [d0, d1]
        ini = initial
        if isinstance(initial, View):
            reads.append(initial)
            ini = initial.ap
        o, a, b = out.ap, d0.ap, d1.ap
        self.op("dve", lambda e: e.tensor_tensor_scan(out=o, data0=a, data1=b, initial=ini,
                                                       op0=ALU.mult, op1=ALU.add), reads, [out])

    def assemble(self):
        self.barrier()
        nc = self.nc
        block = self.es.enter_context(nc.Block())
        sems = self.sems

        def run(e, ops):
            i = 0
            n = len(ops)
            while i < n:
                sc = ops[i][3]
                k = i
                while k < n and ops[k][3] == sc:
                    k += 1
                if PROFILE_SCOPES:
                    cm = nc.named_scope(sc)
                else:
                    cm = ExitStack()
                with cm:
                    for waits, fn, inc, _ in ops[i:k]:
                        for s, v in waits:
                            e.wait_ge(sems[s], v)
                        if fn is not None:
                            ins = fn(e)
                            ins.then_inc(sems[inc[0]], inc[1])
                i = k

        block.tensor(lambda e: run(e, self.ops["pe"]))
        block.scalar(lambda e: run(e, self.ops["act"]))
        block.vector(lambda e: run(e, self.ops["dve"]))
        block.gpsimd(lambda e: run(e, self.ops["pool"]))
        block.sync(lambda e: run(e, self.ops["sp"]))


def bc(ap, axis, shape):
    return ap.unsqueeze(axis).broadcast_to(list(shape))


def pf_layout():
    off = {}
    n = 0

    def add(name, cols):
        nonlocal n
        off[name] = n
        n += cols

    add("cond", 32)
    for i in range(DEPTH):
        add("bada%d" % i, 96)
    for i in range(DEPTH):
        add("g%d" % i, 32)
    for j in range(2):
        add("lcw%d" % j, 32)
        add("lcb%d" % j, 8)
        add("lbr%d" % j, 16)
        add("lbi%d" % j, 16)
        add("llam%d" % j, 16)
        add("lh0%d" % j, 16)
        add("qn%d" % j, 1)
        add("kn%d" % j, 1)
        add("sub%d" % j, 1)
    for j in range(2):
        add("scw%d" % j, 192)
        add("scb%d" % j, 48)
    return off, n


def pr_layout():
    off = {}
    n = 0
    for j in range(2):
        off["subln%d" % j] = n
        n += 128
    for j in range(2):
        off["dtb%d" % j] = n
        n += 128
        off["alog%d" % j] = n
        n += 128
        off["dsk%d" % j] = n
        n += 64
    return off, n


PF_OFF, NPF = pf_layout()
PR_OFF, NPR = pr_layout()


def fm(v):
    v = np.asarray(v, np.float32).reshape(-1, 128)
    return np.ascontiguousarray(v.T)


def blk_fm(W, c0, c1, bw=128):
    K = W.shape[0]
    Wc = W[:, c0:c1]
    nb = (c1 - c0) // bw
    return np.ascontiguousarray(Wc.reshape(K // 128, 128, nb, bw).transpose(2, 1, 0, 3))


def shared_inputs(inp):
    sh = {}
    wa = np.asarray(inp["w_ada"], np.float32).reshape(DEPTH, 16, 128, 24, 4, 128)
    sh["wada"] = np.ascontiguousarray(wa.transpose(0, 3, 2, 4, 1, 5)).reshape(DEPTH * 24, 128, 4 * 16 * 128)
    ewi = np.asarray(inp["even_w_in"], np.float32)
    sh["ewin_fm"] = np.stack([blk_fm(ewi[j], 0, 4096) for j in range(2)]).reshape(2 * 32, 128, 16 * 128)
    sh["ewin_v"] = np.stack([blk_fm(ewi[j], 4096, 5120, 512) for j in range(2)]).reshape(2 * 2, 128, 16 * 512)
    ewo = np.asarray(inp["even_w_out"], np.float32)
    sh["ewout"] = np.stack([blk_fm(ewo[j], 0, 2048) for j in range(2)]).reshape(2 * 16, 128, 16 * 128)
    swi = np.asarray(inp["ssd_w_in"], np.float32)
    sh["swin_fm"] = np.stack([blk_fm(swi[j], 4096, 10240) for j in range(2)]).reshape(2 * 48, 128, 16 * 128)
    sh["swin_z"] = np.stack([blk_fm(swi[j], 0, 4096, 512) for j in range(2)]).reshape(2 * 8, 128, 16 * 512)
    sh["swin_dt"] = np.stack([blk_fm(swi[j], 10240, 10368) for j in range(2)]).reshape(2, 128, 16 * 128)
    swo = np.asarray(inp["ssd_w_out"], np.float32)
    sh["swout"] = np.stack([blk_fm(swo[j], 0, 2048) for j in range(2)]).reshape(2 * 16, 128, 32 * 128)
    fwi = np.asarray(inp["ffn_w_in"], np.float32)
    sh["fwin"] = np.stack([blk_fm(fwi[i], 0, 2 * D_FF) for i in range(DEPTH)]).reshape(DEPTH * 88, 128, 16 * 128)
    fwo = np.asarray(inp["ffn_w_out"], np.float32)
    sh["fwout"] = np.stack([blk_fm(fwo[i], 0, 2048) for i in range(DEPTH)]).reshape(DEPTH * 16, 128, 44 * 128)
    sh["lwr"] = np.ascontiguousarray(np.asarray(inp["lru_w_r"], np.float32).transpose(0, 3, 1, 2, 4)).reshape(2, 128, 2 * 8 * 128)
    sh["lwi"] = np.ascontiguousarray(np.asarray(inp["lru_w_i"], np.float32).transpose(0, 3, 1, 2, 4)).reshape(2, 128, 2 * 8 * 128)
    pr = np.zeros((NPR,), np.float32)
    for j in range(2):
        pr[PR_OFF["subln%d" % j]:][:128] = np.asarray(inp["da_subln"])[j]
        pr[PR_OFF["dtb%d" % j]:][:128] = np.asarray(inp["ssd_dt_bias"])[j].reshape(-1)
        pr[PR_OFF["alog%d" % j]:][:128] = np.asarray(inp["ssd_a_log"])[j].reshape(-1)
        pr[PR_OFF["dsk%d" % j]:][:64] = np.asarray(inp["ssd_d"])[j]
    sh["prow"] = np.ascontiguousarray(np.broadcast_to(pr[None, :], (128, NPR)))
    sh["normw"] = np.ascontiguousarray(np.broadcast_to(np.asarray(inp["ssd_norm_w"], np.float32)[:, None, :], (2, 128, D_INNER)))
    sh["dalam"] = np.asarray(inp["da_lambda"], np.float32).reshape(1, 512)
    cf = np.zeros((128, NCONSTF, 128), np.float32)
    ii = np.arange(128)
    cf[:, C_ID, :] = np.eye(128)
    rot = np.zeros((128, 128), np.float32)
    for m in range(128):
        q = (m % 64) // 16
        if q % 2 == 0:
            rot[m + 16, m] = -1.0
        else:
            rot[m - 16, m] = 1.0
    cf[:, C_ROT, :] = rot
    cf[:, C_MF, :] = (ii[:, None] <= ii[None, :])
    cf[:, C_MB, :] = (ii[:, None] >= ii[None, :])
    cf[:, C_SL, :] = (ii[:, None] > ii[None, :])
    cf[:, C_SU, :] = (ii[:, None] < ii[None, :])
    cf[:, C_ONE, :] = 1.0
    b64 = np.zeros((128, 128), np.float32)
    b64[:64, :64] = 1.0 / 64
    b64[64:, 64:] = 1.0 / 64
    cf[:, C_B64, :] = b64
    sh["constf"] = cf.reshape(128, NCONSTF * 128)
    pos = np.arange(S_LEN)
    row = (pos // GRID_W).astype(np.float32)
    col = (pos % GRID_W).astype(np.float32)
    inv = (10000.0 ** (-np.arange(16, dtype=np.float32) / 16)).astype(np.float32)
    ang = np.concatenate([row[:, None] * inv, row[:, None] * inv, col[:, None] * inv, col[:, None] * inv], -1)
    ang2 = np.concatenate([ang, ang], -1).T.astype(np.float32)
    sh["ropecs"] = np.ascontiguousarray(np.stack([np.cos(ang2), np.sin(ang2)], 1).astype(np.float32)).reshape(128, 2 * S_LEN)
    return sh


def core_inputs(inp, b):
    ci = {}
    xs = np.asarray(inp["x_sample"], np.float32)[b]
    xp = np.asarray(inp["x_prompt"], np.float32)
    X = np.concatenate([xs, xp[2 * b], xp[2 * b + 1]], 0)
    ci["xT"] = np.ascontiguousarray(X.T).reshape(16, 128, T)
    pf = np.zeros((128, NPF), np.float32)
    o = PF_OFF["cond"]
    pf[:, o:o + 32].reshape(128, 16, 2)[:, :, 0] = fm(np.asarray(inp["c"])[b])
    pf[:, o:o + 32].reshape(128, 16, 2)[:, :, 1] = fm(np.asarray(inp["c_ctx"]))
    for i in range(DEPTH):
        o = PF_OFF["bada%d" % i]
        pf[:, o:o + 96] = fm(np.asarray(inp["b_ada"])[i])
        o = PF_OFF["g%d" % i]
        pf[:, o:o + 32] = fm(np.asarray(inp["norm_g"])[i].reshape(-1))
    for j in range(2):
        o = PF_OFF["lcw%d" % j]
        pf[:, o:o + 32] = fm(np.asarray(inp["lru_conv_w"])[j].reshape(-1))
        o = PF_OFF["lcb%d" % j]
        pf[:, o:o + 8] = fm(np.asarray(inp["lru_conv_b"])[j])
        for nm, key in (("lbr", "lru_b_r"), ("lbi", "lru_b_i"), ("llam", "lru_lambda")):
            o = PF_OFF["%s%d" % (nm, j)]
            pf[:, o:o + 16] = fm(np.asarray(inp[key])[j].reshape(-1))
        o = PF_OFF["lh0%d" % j]
        pf[:, o:o + 16] = fm(np.asarray(inp["state_lru"])[b, j].reshape(-1))
        pf[:, PF_OFF["qn%d" % j]] = np.asarray(inp["da_q_norm"])[j].reshape(-1)
        pf[:, PF_OFF["kn%d" % j]] = np.asarray(inp["da_k_norm"])[j].reshape(-1)
        pf[:, PF_OFF["sub%d" % j]] = np.asarray(inp["da_subln"])[j]
        o = PF_OFF["scw%d" % j]
        pf[:, o:o + 192] = fm(np.asarray(inp["ssd_conv_w"])[j].reshape(-1))
        o = PF_OFF["scb%d" % j]
        pf[:, o:o + 48] = fm(np.asarray(inp["ssd_conv_b"])[j])
    ci["pfm"] = pf
    ck = np.asarray(inp["cache_attn_k"], np.float32)[b]
    ci["kcT"] = np.ascontiguousarray(ck.transpose(0, 2, 3, 1)).reshape(16, 128, 256)
    ci["vc"] = np.ascontiguousarray(np.asarray(inp["cache_attn_v"], np.float32)[b]).reshape(2, 256, 1024)
    ss = np.asarray(inp["state_ssd"], np.float32)[b]
    ci["ssd0T"] = np.ascontiguousarray(ss.transpose(0, 1, 4, 2, 3)).reshape(4, 128, 4096)
    return ci


IN_SHAPES = {
    "xT": [16, 128, T], "pfm": [128, NPF], "kcT": [16, 128, 256], "vc": [2, 256, 1024], "ssd0T": [4, 128, 4096],
    "wada": [DEPTH * 24, 128, 8192], "ewin_fm": [64, 128, 2048], "ewin_v": [4, 128, 8192], "ewout": [32, 128, 2048],
    "swin_fm": [96, 128, 2048], "swin_z": [16, 128, 8192], "swin_dt": [2, 128, 2048], "swout": [32, 128, 4096],
    "fwin": [DEPTH * 88, 128, 2048], "fwout": [DEPTH * 16, 128, 5632], "lwr": [2, 128, 2048], "lwi": [2, 128, 2048],
    "prow": [128, NPR], "normw": [2, 128, D_INNER], "dalam": [1, 512], "constf": [128, NCONSTF * 128],
    "ropecs": [128, 2 * S_LEN],
}
OUT_SHAPES = {
    "yT": [16, 128, T], "ok": [2, 2, 256, 1024], "ov": [2, 2, 256, 1024], "olru": [2, 128, 32],
    "ossd": [8, 128, 4096],
}


def build_program(n_layers=DEPTH, debug=False, stop_after=None):
    nc = bass.Bass("TRN2", target_bir_lowering=False)
    es = ExitStack()
    P = Prog(nc, es)
    IN = {k: Tile(nc.dram_tensor(k, v, F32, kind="ExternalInput").ap()) for k, v in IN_SHAPES.items()}
    OUT = {k: Tile(nc.dram_tensor(k, v, F32, kind="ExternalOutput").ap()) for k, v in OUT_SHAPES.items()}
    skind = "ExternalOutput" if debug else "Internal"
    XRES = Tile(nc.dram_tensor("XRES", [16, 128, T], F32, kind=skind).ap(), 32)
    UFM = Tile(nc.dram_tensor("UFM", [48, 128, T], F32, kind=skind).ap(), 48)
    UTM = Tile(nc.dram_tensor("UTM", [T, 4224], F32, kind=skind).ap(), 9)
    MIX = Tile(nc.dram_tensor("MIX", [32, 128, T], BF16, kind=skind).ap(), 32)
    FF = Tile(nc.dram_tensor("FF", [FFC, 128, T], BF16, kind=skind).ap(), FFC)

    PF = P.sb(es, "PF", [128, NPF], F32)
    PR = P.sb(es, "PR", [128, NPR], F32)
    CF = P.sb(es, "CF", [128, NCONSTF, 128], F32)
    CB = P.sb(es, "CB", [128, 5, 128], BF16)
    ADA = P.sb(es, "ADA", [128, DEPTH, 96, 2], F32)
    MODA = P.sb(es, "MODA", [128, DEPTH, 2, 2, 16], F32)
    CNEG = P.sb(es, "CNEG", [128, 2, 16], F32)
    NLAM = P.sb(es, "NLAM", [128, 2], F32)
    SILC = P.sb(es, "SILC", [128, 16, 2], BF16)
    DAL = P.sb(es, "DAL", [1, 512], F32)
    DAL2 = P.sb(es, "DAL2", [1, 8], F32)
    PS = [Tile(es.enter_context(nc.psum_tensor("ps%d" % i, [128, 512], F32))) for i in range(8)]
    psrr = [0]

    def nextps(lo=0, hi=8):
        p = PS[lo + psrr[0] % (hi - lo)]
        psrr[0] += 1
        return p

    evrr = [0]

    def evac(out, in_):
        eng = "act" if evrr[0] % 2 == 0 else "dve"
        evrr[0] += 1
        P.copy(eng, out, in_)

    def pfc(name, a, b=None):
        o = PF_OFF[name]
        if b is None:
            return PF.v(PF.t[:, o + a:o + a + 1])
        return PF.v(PF.t[:, o + a:o + b])

    def prc(name, a, b):
        o = PR_OFF[name]
        return PR.v(PR.t[:, o + a:o + b])

    def cfm(idx):
        return CF.v(CF.t[:, idx, :])

    IDB = CB.v(CB.t[:, 0, :])
    ONESB = CB.v(CB.t[:, 1, :])
    B64B = CB.v(CB.t[:, 2, :])

    P.dma("sp", PF[:], IN["pfm"][:])
    P.dma("sp", PR[:], IN["prow"][:])
    P.dma("sp", CF.v(CF.t[:].rearrange("p a b -> p (a b)")), IN["constf"][:])
    P.dma("sp", DAL[:], IN["dalam"][:])
    P.copy("dve", IDB, cfm(C_ID))
    P.memset("dve", ONESB, 1.0)
    P.copy("dve", B64B, cfm(C_B64))
    P.copy("dve", CB.v(CB.t[:, 3, :]), cfm(C_SL))
    P.copy("dve", CB.v(CB.t[:, 4, :]), cfm(C_SU))

    def rsqrt_inplace(v, scale, bias=EPS):
        P.act(v, v, AF.Sqrt, bias=bias, scale=scale)
        P.recip(v, v)

    def ada_phase():
        P.scope = "ada"
        with ExitStack() as ph:
            WS = [P.sb(ph, "wada", [128, 4, 16, 128], BF16) for _ in range(3)]
            P.act(SILC.v(SILC.t[:].rearrange("p k c -> p (k c)")), pfc("cond", 0, 32), AF.Silu)
            n = 0
            for i in range(n_layers):
                ps = PS[i % 8]
                for bg in range(24):
                    w = WS[n % 3]
                    n += 1
                    P.dma("pool", w.v(w.t[:].rearrange("p a k j -> p (a k j)")), IN["wada"].v(IN["wada"].t[i * 24 + bg]))
                    for blk in range(4):
                        jc = (bg * 4 + blk) * 2
                        for kc in range(16):
                            P.mm(ps.v(ps.t[:, jc:jc + 2]), w.v(w.t[:, blk, kc, :]), SILC.v(SILC.t[:, kc, :]),
                                 start=(kc == 0), stop=(kc == 15))
                psv = ps.t[:, 0:192].rearrange("p (j c) -> p j c", c=2)
                for c in range(2):
                    P.tt("dve", ADA.v(ADA.t[:, i, :, c]), ps.v(psv[:, :, c]), pfc("bada%d" % i, 0, 96), ALU.add)
                for s in range(2):
                    for c in range(2):
                        P.stt("dve", MODA.v(MODA.t[:, i, s, c, :]), ADA.v(ADA.t[:, i, (3 * s + 1) * 16:(3 * s + 2) * 16, c]),
                              1.0, pfc("g%d" % i, s * 16, (s + 1) * 16), ALU.add, ALU.mult)
        P.barrier()

    def ada_col(i, which, kc, c):
        return ADA.v(ADA.t[:, i, which * 16 + kc:which * 16 + kc + 1, c])

    def norm_phase(src, i, s, H):
        P.scope = "L%d_norm%d" % (i, s)
        with ExitStack() as ph:
            XT = [P.sb(ph, "nx", [128, 16, 512], F32, ntok=4) for _ in range(2)]
            SQ = [P.sb(ph, "nsq", [128, 16, 512], BF16) for _ in range(2)]
            RS = [P.sb(ph, "nrs", [128, 512], F32) for _ in range(2)]
            TM = [P.sb(ph, "ntm", [128, 512], F32) for _ in range(4)]
            srcv = src.t.rearrange("c p t -> p c t")

            def load(t):
                x = XT[t % 2]
                for q in range(4):
                    P.dma("sp", x.v(x.t[:, q * 4:(q + 1) * 4, :], q), src.v(srcv[:, q * 4:(q + 1) * 4, t * 512:(t + 1) * 512]))

            load(0)
            for t in range(NTILE):
                if t + 1 < NTILE:
                    load(t + 1)
                c = 0 if t < 4 else 1
                x, sq, rs = XT[t % 2], SQ[t % 2], RS[t % 2]
                for q in range(4):
                    P.act(sq.v(sq.t[:, q * 4:(q + 1) * 4, :]), x.v(x.t[:, q * 4:(q + 1) * 4, :], q), AF.Square)
                ps = nextps()
                for kc in range(16):
                    P.mm(ps[:], ONESB, sq.v(sq.t[:, kc, :]), start=(kc == 0), stop=(kc == 15))
                P.act(rs[:], ps[:], AF.Ln, bias=EPS, scale=1.0 / D)
                P.act(rs[:], rs[:], AF.Exp, scale=-0.5)
                for kc in range(16):
                    tm = TM[kc % 4]
                    P.tt("dve", tm[:], x.v(x.t[:, kc, :], kc // 4), rs[:], ALU.mult)
                    P.act(H.v(H.t[:, kc, t * 512:(t + 1) * 512], t), tm[:], AF.Identity,
                          bias=ada_col(i, 3 * s, kc, c), scale=MODA.v(MODA.t[:, i, s, c, kc:kc + 1]))
        P.barrier()

    def gemm_fm(R, kcn, tiles, groups, wslots, epilogue, rtok=None):
        if rtok is None:
            rtok = lambda kc, tk: tk
        nws = 0
        for gi, grp in enumerate(groups):
            ws = []
            for wv in grp:
                w = wslots[nws % len(wslots)]
                nws += 1
                P.dma("pool", w.v(w.t[:].rearrange("p k j -> p (k j)")), wv)
                ws.append(w)
            for ti, (off, n, tk) in enumerate(tiles):
                pss = []
                for w in ws:
                    ps = nextps()
                    for kc in range(kcn):
                        P.mm(ps.v(ps.t[:, 0:n]), w.v(w.t[:, kc, :]), R.v(R.t[:, kc, off:off + n], rtok(kc, tk)),
                             start=(kc == 0), stop=(kc == kcn - 1))
                    pss.append(ps)
                epilogue(gi, ti, pss)

    def gemm_tm(R, kcn, groups, wslots, epilogue):
        for gi, (wv, bw) in enumerate(groups):
            w = wslots[gi % len(wslots)]
            P.dma("pool", w.v(w.t[:, :, 0:bw]), View(wv.ap.rearrange("p (k j) -> p k j", j=bw), wv.toks))
            for tb in range(NBLK):
                ps = nextps()
                for kc in range(kcn):
                    P.mm(ps.v(ps.t[:, 0:bw]), R.v(R.t[:, kc, tb * 128:(tb + 1) * 128], tb // 4), w.v(w.t[:, kc, 0:bw]),
                         start=(kc == 0), stop=(kc == kcn - 1))
                epilogue(gi, tb, ps)

    TILES5 = [(t * 512, 512, t) for t in range(NTILE)]
    cur_layer = [0]

    def proj_phase(H, fm_w, fm_chunk0, tm_list):
        P.scope = "L%d_proj" % cur_layer[0]
        with ExitStack() as ph:
            WS = [P.sb(ph, "wfm", [128, 16, 128], BF16) for _ in range(4)]
            ROW = [P.sb(ph, "urow", [128, T], F32) for _ in range(2)]

            def epi(gi, ti, pss):
                row = ROW[gi % 2]
                evac(row.v(row.t[:, ti * 512:(ti + 1) * 512]), pss[0][:])
                if ti == NTILE - 1:
                    ch = fm_chunk0 + gi
                    P.dma("sp", UFM.v(UFM.t[ch], ch), row[:])

            gemm_fm(H, 16, TILES5, [[wv] for wv in fm_w], WS, epi)
            P.barrier()
        with ExitStack() as ph:
            WT = [P.sb(ph, "wtm", [128, 16, 512], BF16) for _ in range(2)]
            ST = [P.sb(ph, "ust", [128, 512], F32) for _ in range(4)]
            cnt = [0]

            def epi2(gi, tb, ps):
                wv, bw, col0, tk = tm_list[gi]
                st = ST[cnt[0] % 4]
                cnt[0] += 1
                evac(st.v(st.t[:, 0:bw]), ps.v(ps.t[:, 0:bw]))
                P.dma("sp", UTM.v(UTM.t[tb * 128:(tb + 1) * 128, col0:col0 + bw], tk), st.v(st.t[:, 0:bw]))

            gemm_tm(H, 16, [(wv, bw) for (wv, bw, col0, tk) in tm_list], WT, epi2)
        P.barrier()

    def out_phase(kcn, nchunks_in, wsrc, res_src, res_dst, i, which):
        src, w_in, wbase = wsrc
        P.scope = "L%d_out%d" % (i, which)
        groups_tok = [[(0, 512, 0), (512, 512, 0), (1024, 256, 0)],
                      [(0, 512, 0), (512, 256, 0), (768, 512, 1)]]
        with ExitStack() as ph:
            R = P.sb(ph, "outR", [128, kcn, 1280], BF16, ntok=kcn)
            WS = [P.sb(ph, "wout", [128, kcn, 128], BF16) for _ in range(3)]
            XO = [P.sb(ph, "xold", [128, 1280], F32) for _ in range(2)]
            XN = [P.sb(ph, "xnew", [128, 1280], F32) for _ in range(2)]
            for tg in range(2):
                t0 = tg * 1280
                for kc in range(kcn):
                    P.dma("sp", R.v(R.t[:, kc, :], kc), src.v(src.t[kc, :, t0:t0 + 1280], kc))
                tiles = [(off, n, 0) for (off, n, c) in groups_tok[tg]]

                def epi(gi, ti, pss, tg=tg, t0=t0):
                    off, n, c = groups_tok[tg][ti]
                    xo, xn = XO[gi % 2], XN[gi % 2]
                    if ti == 0:
                        P.dma("sp", xo[:], res_src.v(res_src.t[gi, :, t0:t0 + 1280], gi * 2 + tg))
                    P.stt("dve", xn.v(xn.t[:, off:off + n]), pss[0].v(pss[0].t[:, 0:n]), ada_col(i, which, gi, c),
                          xo.v(xo.t[:, off:off + n]), ALU.mult, ALU.add)
                    if ti == 2:
                        P.dma("sp", res_dst.v(res_dst.t[gi, :, t0:t0 + 1280], gi * 2 + tg), xn[:])

                gemm_fm(R, kcn, tiles, [[w_in.v(w_in.t[wbase + ob])] for ob in range(16)], WS, epi,
                        rtok=lambda kc, tk: kc)
        P.barrier()

    def ffn_in_phase(H, i):
        P.scope = "L%d_ffnin" % i
        with ExitStack() as ph:
            WS = [P.sb(ph, "wff", [128, 16, 128], BF16) for _ in range(6)]
            SG = [P.sb(ph, "sg", [128, 512], F32) for _ in range(3)]
            ROW = [P.sb(ph, "frow", [128, T], BF16) for _ in range(2)]
            fw = IN["fwin"]
            cnt = [0]

            def epi(gi, ti, pss):
                sg = SG[cnt[0] % 3]
                cnt[0] += 1
                row = ROW[gi % 2]
                P.act(sg[:], pss[0][:], AF.Silu)
                P.tt("dve", row.v(row.t[:, ti * 512:(ti + 1) * 512]), sg[:], pss[1][:], ALU.mult)
                if ti == NTILE - 1:
                    P.dma("sp", FF.v(FF.t[gi], gi), row[:])

            groups = [[fw.v(fw.t[i * 88 + j]), fw.v(fw.t[i * 88 + 44 + j])] for j in range(FFC)]
            gemm_fm(H, 16, TILES5, groups, WS, epi)
        P.barrier()

    def setup_small():
        P.scope = "small"
        with ExitStack() as ph:
            A1 = P.sb(ph, "a1", [128, 16], F32)
            A2 = P.sb(ph, "a2", [128, 16], F32)
            for j in range(2):
                lam = pfc("llam%d" % j, 0, 16)
                P.stt("dve", A1[:], lam, -1.0, lam, ALU.mult, ALU.max)
                P.act(A1[:], A1[:], AF.Exp, scale=-1.0)
                P.act(A1[:], A1[:], AF.Ln, bias=1.0, scale=1.0)
                P.ts("dve", A2[:], lam, -1.0, 0.0, ALU.mult, ALU.max)
                P.tt("dve", A2[:], A2[:], A1[:], ALU.add)
                P.ts("dve", CNEG.v(CNEG.t[:, j, :]), A2[:], -8.0, None, ALU.mult)
            TMPL = P.sb(ph, "tmpl", [1, 64], F32)
            for j in range(2):
                lam_init = 0.8 - 0.6 * math.exp(-0.3 * (2 * j))
                for k in range(2):
                    o = j * 256 + k * 128
                    P.tt("dve", TMPL[:], DAL.v(DAL.t[:, o:o + 64]), DAL.v(DAL.t[:, o + 64:o + 128]), ALU.mult)
                    acc = DAL2.v(DAL2.t[:, j * 4 + k:j * 4 + k + 1])
                    P.memset("dve", acc, 0.0)
                    P.act(TMPL[:], TMPL[:], AF.Identity, accum_out=acc)
                    P.act(acc, acc, AF.Exp)
                res = DAL2.v(DAL2.t[:, j * 4 + 2:j * 4 + 3])
                P.tt("dve", res, DAL2.v(DAL2.t[:, j * 4 + 1:j * 4 + 2]), DAL2.v(DAL2.t[:, j * 4:j * 4 + 1]), ALU.subtract)
                P.ts("dve", res, res, -lam_init, None, ALU.add)
                ps = nextps()
                P.mm(ps.v(ps.t[:, 0:1]), CF.v(CF.t[0:1, C_ONE, :]), res)
                P.copy("dve", NLAM.v(NLAM.t[:, j:j + 1]), ps.v(ps.t[:, 0:1]))
        P.barrier()

    def conv4(dst, src, wcol, bcol):
        for si, (s0, s1) in enumerate(SEGS):
            eng = "dve"
            P.ts(eng, dst.v(dst.t[:, s0:s1], si), src.v(src.t[:, s0:s1]), wcol(2), bcol, ALU.mult, ALU.add)
            P.stt(eng, dst.v(dst.t[:, s0 + 2:s1], si), src.v(src.t[:, s0:s1 - 2]), wcol(0), dst.v(dst.t[:, s0 + 2:s1], si), ALU.mult, ALU.add)
            P.stt(eng, dst.v(dst.t[:, s0 + 1:s1], si), src.v(src.t[:, s0:s1 - 1]), wcol(1), dst.v(dst.t[:, s0 + 1:s1], si), ALU.mult, ALU.add)
            P.stt(eng, dst.v(dst.t[:, s0:s1 - 1], si), src.v(src.t[:, s0 + 1:s1]), wcol(3), dst.v(dst.t[:, s0:s1 - 1], si), ALU.mult, ALU.add)

    def lru_phase(j):
        P.scope = "L%d_lru" % cur_layer[0]
        with ExitStack() as ph:
            LW = P.sb(ph, "lw", [128, 2, 2, 8, 128], BF16, ntok=2)
            P.dma("pool", LW.v(LW.t[:, 0].rearrange("p d k j -> p (d k j)"), 0), IN["lwr"].v(IN["lwr"].t[j]))
            P.dma("pool", LW.v(LW.t[:, 1].rearrange("p d k j -> p (d k j)"), 1), IN["lwi"].v(IN["lwi"].t[j]))
            GA = [P.sb(ph, "ga", [128, T], F32) for _ in range(2)]
            XR = [P.sb(ph, "xr", [128, T], F32) for _ in range(2)]
            XC = P.sb(ph, "xc", [128, T], F32, ntok=3)
            XCb = P.sb(ph, "xcb", [128, T], BF16)
            RA = P.sb(ph, "ra", [128, T], F32)
            GB = P.sb(ph, "gb", [128, T], F32)
            HF = P.sb(ph, "hf", [128, T], F32)
            HB = P.sb(ph, "hb", [128, T], F32)
            REC = [P.sb(ph, "rec", [128, T], BF16) for _ in range(2)]
            FS = P.sb(ph, "fs", [128, 8, 2, 2], F32)

            def load(cc):
                P.dma("sp", GA[cc % 2][:], UFM.v(UFM.t[cc], cc))
                P.dma("sp", XR[cc % 2][:], UFM.v(UFM.t[8 + cc], 8 + cc))

            load(0)
            for cc in range(8):
                if cc + 1 < 8:
                    load(cc + 1)
                ga, xr = GA[cc % 2], XR[cc % 2]
                P.act(ga[:], ga[:], AF.Gelu_apprx_tanh)
                conv4(XC, xr, lambda k: pfc("lcw%d" % j, k * 8 + cc), pfc("lcb%d" % j, cc))
                P.copy("act", XCb[:], XC[:])
                for d in range(2):
                    HS = HF if d == 0 else HB
                    for t in range(NTILE):
                        sl = slice(t * 512, (t + 1) * 512)
                        ps_r = nextps()
                        P.mm(ps_r[:], LW.v(LW.t[:, 0, d, cc, :], 0), XCb.v(XCb.t[:, sl]))
                        ps_i = nextps()
                        P.mm(ps_i[:], LW.v(LW.t[:, 1, d, cc, :], 1), XCb.v(XCb.t[:, sl]))
                        P.act(RA.v(RA.t[:, sl]), ps_r[:], AF.Sigmoid, bias=pfc("lbr%d" % j, d * 8 + cc))
                        P.act(GB.v(GB.t[:, sl]), ps_i[:], AF.Sigmoid, bias=pfc("lbi%d" % j, d * 8 + cc))
                    P.act(RA[:], RA[:], AF.Exp, scale=CNEG.v(CNEG.t[:, j, d * 8 + cc:d * 8 + cc + 1]))
                    P.tt("dve", GB[:], GB[:], XC[:], ALU.mult)
                    P.act(HS[:], RA[:], AF.Square)
                    P.act(HS[:], HS[:], AF.Sqrt, bias=1.0, scale=-1.0)
                    P.tt("dve", GB[:], GB[:], HS[:], ALU.mult)
                    for si, (s0, s1) in enumerate(SEGS):
                        init = pfc("lh0%d" % j, d * 8 + cc) if si == 0 else 0.0
                        if d == 0:
                            P.scan(HS.v(HS.t[:, s0:s1]), RA.v(RA.t[:, s0:s1]), GB.v(GB.t[:, s0:s1]), init)
                        else:
                            P.scan(HS.v(HS.t[:, s0:s1][:, ::-1]), RA.v(RA.t[:, s0:s1][:, ::-1]),
                                   GB.v(GB.t[:, s0:s1][:, ::-1]), init)
                for pi in range(2):
                    s0, s1 = SEGS[1 + pi]
                    P.copy("dve", FS.v(FS.t[:, cc, pi, 0:1]), HF.v(HF.t[:, s1 - 1:s1]))
                    P.copy("dve", FS.v(FS.t[:, cc, pi, 1:2]), HB.v(HB.t[:, s0:s0 + 1]))
                rec = REC[cc % 2]
                P.tt("dve", HF[:], HF[:], HB[:], ALU.add)
                P.tt("dve", rec[:], HF[:], ga[:], ALU.mult)
                P.dma("sp", MIX.v(MIX.t[cc], cc), rec[:])
            P.dma("sp", OUT["olru"].v(OUT["olru"].t[j]), FS.v(FS.t[:].rearrange("p a b c -> p (a b c)")))
        P.barrier()

    def att_phase(j, i):
        P.scope = "L%d_att" % i
        lam_init = 0.8 - 0.6 * math.exp(-0.3 * i)
        with ExitStack() as ph:
            CS = P.sb(ph, "cs", [128, 2, S_LEN], F32)
            P.dma("sp", CS.v(CS.t[:].rearrange("p a t -> p (a t)")), IN["ropecs"][:])
            Q32 = [P.sb(ph, "q32", [128, T], F32) for _ in range(2)]
            K32 = [P.sb(ph, "k32", [128, T], F32) for _ in range(2)]
            V32 = [P.sb(ph, "v32", [128, NBLK, 128], F32, ntok=4) for _ in range(2)]
            QT = P.sb(ph, "qt", [128, T], BF16, ntok=5)
            KT = P.sb(ph, "kt", [128, T + 256], BF16, ntok=6)
            VE = P.sb(ph, "ve", [128, 22, 129], BF16, ntok=4)
            ATTb = [P.sb(ph, "attb", [128, T], BF16) for _ in range(2)]
            SQ = [P.sb(ph, "asq", [128, 512], BF16) for _ in range(2)]
            RT = [P.sb(ph, "art", [128, 512], F32) for _ in range(2)]
            XN = [P.sb(ph, "axn", [128, 512], F32) for _ in range(2)]
            T1 = [P.sb(ph, "at1", [128, 512], F32) for _ in range(2)]
            T2 = [P.sb(ph, "at2", [128, 512], F32) for _ in range(2)]
            PT = [P.sb(ph, "apt", [128, 512], BF16) for _ in range(4)]
            KO = P.sb(ph, "ako", [128, 4, 128], F32)
            P.memset("dve", VE.v(VE.t[:, :, 128:129], 3), 1.0)
            cnt = {"n": 0, "sc": 0, "pt": 0, "rz": 0, "e": 0}

            def load(hd):
                P.dma("sp", Q32[hd % 2][:], UFM.v(UFM.t[16 + hd], 16 + hd))
                P.dma("sp", K32[hd % 2][:], UFM.v(UFM.t[24 + hd], 24 + hd))
                v = V32[hd % 2]
                src = UTM.t[:, hd * 128:(hd + 1) * 128].rearrange("(b p) c -> p b c", p=128)
                for q in range(4):
                    P.dma("sp", v.v(v.t[:, q * 5:(q + 1) * 5, :], q), UTM.v(src[:, q * 5:(q + 1) * 5, :], hd // 4))

            def qknorm(src, dst, gname, t, is_k, hd):
                n = cnt["n"]
                cnt["n"] += 1
                sl = slice(t * 512, (t + 1) * 512)
                sq, rt, xn = SQ[n % 2], RT[n % 2], XN[n % 2]
                P.act(sq[:], src.v(src.t[:, sl]), AF.Square)
                ps = PS[7]
                P.mm(ps[:], B64B, sq[:])
                P.act(rt[:], ps[:], AF.Ln, bias=EPS, scale=1.0)
                P.act(rt[:], rt[:], AF.Exp, scale=-0.5)
                P.stt("dve", xn[:], src.v(src.t[:, sl]), pfc(gname, 0), rt[:], ALU.mult, ALU.mult)
                if t < 4:
                    ps2 = PS[7]
                    P.mm(ps2[:], cfm(C_ROT), xn[:])
                    t1, t2 = T1[n % 2], T2[n % 2]
                    P.tt("dve", t1[:], xn[:], CS.v(CS.t[:, 0, sl]), ALU.mult)
                    P.tt("dve", t2[:], ps2[:], CS.v(CS.t[:, 1, sl]), ALU.mult)
                    P.tt("dve", dst.v(dst.t[:, sl], t), t1[:], t2[:], ALU.add)
                else:
                    P.copy("act", dst.v(dst.t[:, sl], t), xn[:])
                    if is_k:
                        for blk in range(4):
                            ps3 = PS[7]
                            P.transpose(ps3.v(ps3.t[:, 0:128]), xn.v(xn.t[:, blk * 128:(blk + 1) * 128]), cfm(C_ID))
                            evac(KO.v(KO.t[:, blk, :]), ps3.v(ps3.t[:, 0:128]))
                        okv = OUT["ok"].t[j].rearrange("a (b p) c -> p (a b) c", p=128)[:, :, hd * 128:(hd + 1) * 128]
                        P.dma("sp", OUT["ok"].v(okv), KO[:])

            def attend(q0, nq, keys, att):
                qtok = q0 // 512
                nk = len(keys)
                e = cnt["e"]
                cnt["e"] += 1
                par = e % 2
                psO = [PS[4], PS[5]]
                zbuf = [[ZA[par * 4 + m * 2], ZA[par * 4 + m * 2 + 1]] for m in range(2)]
                zacc = [None, None]

                def score(ki):
                    kc0, ktok, vb, vtok = keys[ki]
                    banks = [PS[(cnt["sc"] % 2) * 2], PS[(cnt["sc"] % 2) * 2 + 1]]
                    cnt["sc"] += 1
                    for m in range(2):
                        P.mm(banks[m].v(banks[m].t[:, 0:nq]), KT.v(KT.t[m * 64:(m + 1) * 64, kc0:kc0 + 128], ktok),
                             QT.v(QT.t[m * 64:(m + 1) * 64, q0:q0 + nq], qtok))
                    return banks

                pend = [score(0)]
                for ki in range(nk):
                    kc0, ktok, vb, vtok = keys[ki]
                    banks = pend.pop(0)
                    pts = []
                    for m in range(2):
                        pt = PT[cnt["pt"] % 4]
                        cnt["pt"] += 1
                        P.act(pt.v(pt.t[:, 0:nq]), banks[m].v(banks[m].t[:, 0:nq]), AF.Exp, scale=0.125)
                        pts.append(pt)
                    if ki + 1 < nk:
                        pend.append(score(ki + 1))
                    if ki == min(1, nk - 1) and defer:
                        for f in defer:
                            f()
                        del defer[:]
                    for m in range(2):
                        P.mm(psO[m].v(psO[m].t[:, 0:nq]), VE.v(VE.t[:, vb, 0:128], vtok), pts[m].v(pts[m].t[:, 0:nq]),
                             start=(ki == 0), stop=(ki == nk - 1))
                    for m, eng in ((0, "dve"), (1, EW2)):
                        z = zbuf[m][ki % 2]
                        if ki == 0:
                            P.copy(eng, z.v(z.t[:, 0:nq]), pts[m].v(pts[m].t[:, 0:nq]))
                        else:
                            zo = zbuf[m][(ki - 1) % 2]
                            P.tt(eng, z.v(z.t[:, 0:nq]), zo.v(zo.t[:, 0:nq]), pts[m].v(pts[m].t[:, 0:nq]), ALU.add)
                        zacc[m] = z
                t0, t1 = TT[par * 2], TT[par * 2 + 1]
                for m, t in enumerate((t0, t1)):
                    P.copy("dve", t.v(t.t[:, 0:nq]), psO[m].v(psO[m].t[:, 0:nq]))

                def epilogue():
                    r = [RR[par * 2], RR[par * 2 + 1]]
                    for m in range(2):
                        psz = PS[6]
                        P.mm(psz.v(psz.t[:, 0:nq]), cfm(C_ONE), zacc[m].v(zacc[m].t[:, 0:nq]))
                        P.act(r[m].v(r[m].t[:, 0:nq]), psz.v(psz.t[:, 0:nq]), AF.Ln)
                        P.act(r[m].v(r[m].t[:, 0:nq]), r[m].v(r[m].t[:, 0:nq]), AF.Exp, scale=-1.0)
                    P.tt("dve", t0.v(t0.t[:, 0:nq]), t0.v(t0.t[:, 0:nq]), r[0].v(r[0].t[:, 0:nq]), ALU.mult)
                    P.tt(EW2, t1.v(t1.t[:, 0:nq]), t1.v(t1.t[:, 0:nq]), r[1].v(r[1].t[:, 0:nq]), ALU.mult)
                    a_, sq = r[0], SQ2[par]
                    P.stt("dve", a_.v(a_.t[:, 0:nq]), t1.v(t1.t[:, 0:nq]), NLAM.v(NLAM.t[:, j:j + 1]), t0.v(t0.t[:, 0:nq]),
                          ALU.mult, ALU.add)
                    P.act(sq.v(sq.t[:, 0:nq]), a_.v(a_.t[:, 0:nq]), AF.Square)
                    ps = PS[7]
                    P.mm(ps.v(ps.t[:, 0:nq]), ONESB, sq.v(sq.t[:, 0:nq]))
                    rt = r[1]
                    P.act(rt.v(rt.t[:, 0:nq]), ps.v(ps.t[:, 0:nq]), AF.Ln, bias=EPS, scale=1.0 / 128)
                    P.act(rt.v(rt.t[:, 0:nq]), rt.v(rt.t[:, 0:nq]), AF.Exp, scale=-0.5)
                    P.stt("dve", att.v(att.t[:, q0:q0 + nq]), a_.v(a_.t[:, 0:nq]), SUBG[:], rt.v(rt.t[:, 0:nq]), ALU.mult, ALU.mult)

                defer.append(epilogue)

            defer = []
            RR = [P.sb(ph, "arr", [128, 512], F32) for _ in range(4)]
            TT = [P.sb(ph, "att_t", [128, 512], F32) for _ in range(4)]
            ZA = [P.sb(ph, "aza", [128, 512], F32) for _ in range(8)]
            SQ2 = [P.sb(ph, "asq2", [128, 512], BF16) for _ in range(2)]
            SUBG = P.sb(ph, "subg", [128, 1], F32)
            P.ts("dve", SUBG[:], pfc("sub%d" % j, 0), 1.0 - lam_init, None, ALU.mult)

            load(0)
            for hd in range(8):
                if hd + 1 < 8:
                    load(hd + 1)
                q32, k32, v32 = Q32[hd % 2], K32[hd % 2], V32[hd % 2]
                att = ATTb[hd % 2]
                P.dma("pool", KT.v(KT.t[:, T:T + 256], 5), IN["kcT"].v(IN["kcT"].t[j * 8 + hd]))
                vcv = IN["vc"].t[j, :, hd * 128:(hd + 1) * 128].rearrange("(b p) c -> p b c", p=128)
                P.dma("pool", VE.v(VE.t[:, 20:22, 0:128], 2), IN["vc"].v(vcv))
                for t in range(NTILE):
                    qknorm(q32, QT, "qn%d" % j, t, False, hd)
                    qknorm(k32, KT, "kn%d" % j, t, True, hd)
                P.copy("act", VE.v(VE.t[:, 0:16, 0:128], 0), v32.v(v32.t[:, 0:16, :]))
                P.copy("dve", VE.v(VE.t[:, 16:20, 0:128], 1), v32.v(v32.t[:, 16:20, :], 3))
                ovv = OUT["ov"].t[j].rearrange("a (b p) c -> p (a b) c", p=128)[:, :, hd * 128:(hd + 1) * 128]
                P.dma("sp", OUT["ov"].v(ovv), v32.v(v32.t[:, 16:20, :], 3))
                skeys = [(kb * 128, kb // 4, kb, 0) for kb in range(16)] + [(T, 5, 20, 2), (T + 128, 5, 21, 2)]
                for qt in range(4):
                    attend(qt * 512, 512, skeys, att)
                for pi in range(2):
                    q0 = S_LEN + pi * P_LEN
                    attend(q0, 256, [(q0, 4, 16 + 2 * pi, 1), (q0 + 128, 4, 17 + 2 * pi, 1)], att)
                for f in defer:
                    f()
                del defer[:]
                P.dma("sp", MIX.v(MIX.t[8 + hd], 8 + hd), att[:])
        P.barrier()

    SEQ_CHUNKS = [list(range(0, 16)), [16, 17], [18, 19]]
    EW2 = os.environ.get("EW2", "pool")
    ssd_dbg = stop_after[1] if (stop_after is not None and stop_after[0] == "ssdonly") else None

    def ssd_phase(j):
        P.scope = "L%d_ssd" % cur_layer[0]
        with ExitStack() as ph:
            DTA = P.sb(ph, "dta", [128, NBLK, 128], F32)
            AALL = P.sb(ph, "aall", [128, NBLK, 128], F32)
            AH = P.sb(ph, "ah", [128, NBLK, 128], BF16)
            AL = P.sb(ph, "al", [128, NBLK, 128], BF16)
            with ExitStack() as p2:
                XX = P.sb(p2, "sx", [128, NBLK, 128], F32)
                EE = P.sb(p2, "se", [128, NBLK, 128], F32)
                AR = P.sb(p2, "sar", [128, 128], F32)
                src = UTM.t[:, 4096:4224].rearrange("(b p) c -> p b c", p=128)
                for q in range(4):
                    P.dma("sp", DTA.v(DTA.t[:, q * 5:(q + 1) * 5, :]), UTM.v(src[:, q * 5:(q + 1) * 5, :], 8))
                dtb = prc("dtb%d" % j, 0, 128)
                P.tt("dve", XX[:], DTA[:], View(bc(dtb.ap, 1, [128, NBLK, 128]), dtb.toks), ALU.add)
                P.stt("dve", EE[:], XX[:], -1.0, XX[:], ALU.mult, ALU.max)
                P.act(EE[:], EE[:], AF.Exp, scale=-1.0)
                P.act(EE[:], EE[:], AF.Ln, bias=1.0, scale=1.0)
                P.stt("dve", DTA[:], XX[:], 0.0, EE[:], ALU.max, ALU.add)
                P.act(AR[:], prc("alog%d" % j, 0, 128), AF.Exp)
                P.stt("dve", AALL[:], DTA[:], -1.0, View(bc(AR.t[:], 1, [128, NBLK, 128]), AR.toks), ALU.mult, ALU.mult)
                P.copy("dve", AH[:], AALL[:])
                P.tt("dve", XX[:], AALL[:], AH[:], ALU.subtract)
                P.copy("dve", AL[:], XX[:])
            P.barrier()
            if ssd_dbg is not None and ssd_dbg == 0:
                return
            for g in range(8 if (ssd_dbg is None or ssd_dbg >= 4) else 1):
                ssd_group(j, g, DTA, AALL, AH, AL)
        P.barrier()

    def ssd_group(j, g, DTA, AALL, AH, AL):
        with ExitStack() as ph:
            BTf = P.sb(ph, "btf", [128, T], BF16)
            CTf = P.sb(ph, "ctf", [128, T], BF16)
            BTM = P.sb(ph, "btm", [128, NBLK, 128], BF16)
            XTM = P.sb(ph, "xtm", [128, NBLK, 512], BF16)
            SENTS = P.sb(ph, "sents", [128, NBLK, 512], BF16, ntok=NBLK)
            YGT = P.sb(ph, "ygt", [128, 4, T], BF16)
            EL = P.sb(ph, "el", [128, 2, NBLK, 8], F32)
            DEC = P.sb(ph, "dec", [128, 2, NBLK, 8], F32)
            ETOT = P.sb(ph, "etot", [128, 2, NBLK, 8], F32)
            S0 = P.sb(ph, "s0", [128, 2, 512], F32)
            NW = P.sb(ph, "nw", [128, 512], F32)
            P.dma("sp", NW[:], IN["normw"].v(IN["normw"].t[j, :, g * 512:(g + 1) * 512]))
            for d in range(2):
                P.dma("sp", S0.v(S0.t[:, d, :]), IN["ssd0T"].v(IN["ssd0T"].t[j * 2 + d, :, g * 512:(g + 1) * 512]))
            with ExitStack() as p2:
                RAW = [P.sb(p2, "raw", [128, T], F32) for _ in range(2)]
                CVs = [P.sb(p2, "cv", [128, T], F32, ntok=3) for _ in range(2)]
                XSbs = [P.sb(p2, "xsb", [128, T], BF16) for _ in range(2)]
                ACS = P.sb(p2, "acs", [128, 2, NBLK, 16], F32)
                TD = P.sb(p2, "td", [128, 2, NBLK, 8], F32)
                chunks = [32 + g, 40 + g] + [g * 4 + i for i in range(4)]

                def load(k):
                    P.dma("sp", RAW[k % 2][:], UFM.v(UFM.t[chunks[k]], chunks[k]))

                load(0)
                for k, ch in enumerate(chunks):
                    if k + 1 < len(chunks):
                        load(k + 1)
                    CV = CVs[k % 2]
                    conv4(CV, RAW[k % 2], lambda tap, ch=ch: pfc("scw%d" % j, tap * 48 + ch), pfc("scb%d" % j, ch))
                    dst = BTf if k == 0 else (CTf if k == 1 else XSbs[k % 2])
                    P.act(dst[:], CV[:], AF.Silu)
                    if k == 1:
                        continue
                    for c4 in range(NBLK // 4):
                        pst = PS[7]
                        pb = pst.t[:].bitcast(BF16)
                        for cc in range(4):
                            c = c4 * 4 + cc
                            P.transpose(pst.v(pb[:, cc * 128:(cc + 1) * 128]), dst.v(dst.t[:, c * 128:(c + 1) * 128]), IDB)
                        srcv = pst.v(pb[:, 0:512].rearrange("p (a b) -> p a b", a=4))
                        if k == 0:
                            evac(BTM.v(BTM.t[:, c4 * 4:(c4 + 1) * 4, :]), srcv)
                        else:
                            i = k - 2
                            evac(XTM.v(XTM.t[:, c4 * 4:(c4 + 1) * 4, i * 128:(i + 1) * 128]), srcv)
                for d in range(2):
                    ps = PS[d]
                    for c in range(NBLK):
                        rhs = AALL.v(AALL.t[:, c, d * 64 + g * 8:d * 64 + g * 8 + 8])
                        P.mm(ps.v(ps.t[:, c * 16:c * 16 + 8]), cfm(C_MF if d == 0 else C_MB), rhs)
                        P.mm(ps.v(ps.t[:, c * 16 + 8:c * 16 + 16]), cfm(C_ONE), rhs)
                    P.copy("act", ACS.v(ACS.t[:, d].rearrange("p c k -> p (c k)")), ps.v(ps.t[:, 0:NBLK * 16]))
                    P.act(EL.v(EL.t[:, d]), ACS.v(ACS.t[:, d, :, 0:8]), AF.Exp)
                    P.act(ETOT.v(ETOT.t[:, d]), ACS.v(ACS.t[:, d, :, 8:16]), AF.Exp)
                    P.tt("dve", TD.v(TD.t[:, d]), ACS.v(ACS.t[:, d, :, 8:16]), ACS.v(ACS.t[:, d, :, 0:8]), ALU.subtract)
                    P.act(DEC.v(DEC.t[:, d]), TD.v(TD.t[:, d]), AF.Exp)
            P.barrier()
            if ssd_dbg is not None and ssd_dbg == 1:
                return
            with ExitStack() as p3:
                SENT = P.sb(p3, "sent", [128, 512], F32)
                SENTb = P.sb(p3, "sentb", [128, 512], BF16)
                XTd = [P.sb(p3, "xtd", [128, 512], BF16) for _ in range(4)]
                XD = [P.sb(p3, "xd", [128, 512], BF16) for _ in range(2)]
                CBM = [P.sb(p3, "cbm", [128, 2, 128], F32) for _ in range(2)]
                AU = [P.sb(p3, "au", [128, 2, 8, 128], BF16) for _ in range(4)]
                EXPD = [P.sb(p3, "expd", [128, 8, 128], F32) for _ in range(4)]
                MT = [P.sb(p3, "mt", [128, 8, 128], BF16) for _ in range(4)]
                YO = [P.sb(p3, "yo", [128, 512], F32) for _ in range(2)]
                YY = [P.sb(p3, "yy", [128, 512], F32) for _ in range(2)]
                XS2 = [P.sb(p3, "xs2", [128, 512], F32) for _ in range(2)]
                ZT = [P.sb(p3, "zt", [128, 512], F32) for _ in range(2)]
                JK = P.sb(p3, "sjk", [128, 512], F32)
                YN = [P.sb(p3, "yn", [128, 512], BF16) for _ in range(2)]
                SS = [P.sb(p3, "sss", [128, 1], F32) for _ in range(2)]

                def h8(tile_view_ap):
                    return tile_view_ap.rearrange("p (h q) -> p h q", h=8)

                def hb8(tl, d, c):
                    return View(tl.t[:, d, c, :].unsqueeze(2).broadcast_to([128, 8, 64]), tl.toks)

                def dt8(c, d):
                    ap = DTA.t[:, c, d * 64 + g * 8:d * 64 + g * 8 + 8]
                    return View(ap.unsqueeze(2).broadcast_to([128, 8, 64]), DTA.toks)

                def make_xd(c, d, xt):
                    xd = XD[c % 2]
                    P.tt(EW2, View(h8(xd.t[:]), xd.toks), View(h8(xt.t[:]), xt.toks), hb8(DEC, d, c), ALU.mult)
                    return xd

                def state_update(S, c, d, xd):
                    psS = PS[6]
                    P.mm(psS[:], BTM.v(BTM.t[:, c, :]), xd[:])
                    P.tt("dve", View(h8(S.t[:]), S.toks), View(h8(S.t[:]), S.toks), hb8(ETOT, d, c), ALU.mult)
                    P.tt("dve", S[:], S[:], psS[:], ALU.add)

                for si, chs in enumerate(SEQ_CHUNKS):
                    if si == 0:
                        P.copy("dve", SENT[:], S0.v(S0.t[:, 0, :]))
                    else:
                        P.memset("dve", SENT[:], 0.0)
                    for c in chs:
                        P.copy("act", SENTS.v(SENTS.t[:, c, :], c), SENT[:])
                        xt = XTd[c % 2]
                        P.tt(EW2, View(h8(xt.t[:]), xt.toks), XTM.v(h8(XTM.t[:, c, :])), dt8(c, 0), ALU.mult)
                        state_update(SENT, c, 0, make_xd(c, 0, xt))
                    if si > 0:
                        P.dma("sp", OUT["ossd"].v(OUT["ossd"].t[(j * 2 + si - 1) * 2 + 0, :, g * 512:(g + 1) * 512]), SENT[:])
                if ssd_dbg is not None and ssd_dbg == 2:
                    return
                dsk = prc("dsk%d" % j, g * 8, g * 8 + 8)
                dskb = View(dsk.ap.unsqueeze(2).broadcast_to([128, 8, 64]), dsk.toks)
                order = [(si, c) for si, chs in enumerate(SEQ_CHUNKS) for c in reversed(chs)]

                def loadz(k):
                    c = order[k][1]
                    P.dma("sp", ZT[k % 2][:], UTM.v(UTM.t[c * 128:(c + 1) * 128, g * 512:(g + 1) * 512], g))

                def common(k):
                    c = order[k][1]
                    par = k % 2
                    st = {"c": c, "par": par, "si": order[k][0], "xts": [None, None], "mts": [None, None]}
                    cbm = CBM[par]
                    psC = PS[7]
                    pcv = psC.v(psC.t[:, 256:384])
                    P.mm(pcv, BTf.v(BTf.t[:, c * 128:(c + 1) * 128]), CTf.v(CTf.t[:, c * 128:(c + 1) * 128]))
                    P.tt("dve", cbm.v(cbm.t[:, 0, :]), pcv, cfm(C_MF), ALU.mult)
                    P.tt("dve", cbm.v(cbm.t[:, 1, :]), pcv, cfm(C_MB), ALU.mult)
                    xs2 = XS2[par]
                    P.tt(EW2, View(h8(xs2.t[:]), xs2.toks), XTM.v(h8(XTM.t[:, c, :])), dskb, ALU.mult)
                    st["xs2"] = xs2
                    return st

                def indep_d(st, d):
                    c, par = st["c"], st["par"]
                    cbm = CBM[par]
                    au, expd, mt = AU[2 * par + d], EXPD[2 * par + d], MT[2 * par + d]
                    mask = cfm(C_MF if d == 0 else C_MB)
                    maskb = View(mask.ap.unsqueeze(1).broadcast_to([128, 8, 128]), mask.toks)
                    for x, AX in enumerate((AH, AL)):
                        arow = AX.t[:, c, d * 64 + g * 8:d * 64 + g * 8 + 8]
                        P.tt(EW2, au.v(au.t[:, x]), maskb, View(arow.unsqueeze(2).broadcast_to([128, 8, 128]), AX.toks), ALU.mult)
                    tri = CB.v(CB.t[:, 3 if d == 0 else 4, :])
                    for hh in range(2):
                        psD = PS[2 * d + hh]
                        for x in range(2):
                            P.mm(psD[:], tri, au.v(au.t[:, x, hh * 4:(hh + 1) * 4, :].rearrange("p a b -> p (a b)")),
                                 start=(x == 0), stop=(x == 1))
                        P.act(expd.v(expd.t[:, hh * 4:(hh + 1) * 4, :].rearrange("p a b -> p (a b)")), psD[:], AF.Exp)
                    P.tt("dve", mt[:], expd[:], View(cbm.t[:, d, :].unsqueeze(1).broadcast_to([128, 8, 128]), cbm.toks), ALU.mult)
                    xt = XTd[2 * par + d]
                    P.tt(EW2, View(h8(xt.t[:]), xt.toks), XTM.v(h8(XTM.t[:, c, :])), dt8(c, d), ALU.mult)
                    st["xts"][d] = xt
                    st["mts"][d] = mt
                    if d == 1:
                        st["xd"] = make_xd(c, 1, xt)

                def ymm(st):
                    psY = PS[4]
                    for d in range(2):
                        mt, xt = st["mts"][d], st["xts"][d]
                        for h in range(8):
                            P.mm(psY.v(psY.t[:, h * 64:(h + 1) * 64]), mt.v(mt.t[:, h, :]), xt.v(xt.t[:, h * 64:(h + 1) * 64]),
                                 start=(d == 0 and h == 0), stop=(d == 1), sgc=True)

                def recur_a(k, st):
                    si, c = order[k]
                    first = (k == 0 or order[k - 1][0] != si)
                    last = (k == len(order) - 1 or order[k + 1][0] != si)
                    if first:
                        if si == 0:
                            P.copy("dve", SENT[:], S0.v(S0.t[:, 1, :]))
                        else:
                            P.memset("dve", SENT[:], 0.0)
                    P.copy("act", SENTb[:], SENT[:])
                    for d in range(2):
                        psYo = PS[5]
                        sent_in = SENTS.v(SENTS.t[:, c, :], c) if d == 0 else SENTb[:]
                        P.mm(psYo[:], CTf.v(CTf.t[:, c * 128:(c + 1) * 128]), sent_in)
                        yo = YO[d]
                        P.tt("dve", View(h8(yo.t[:]), yo.toks), View(h8(psYo.t[:]), psYo.toks), hb8(EL, d, c), ALU.mult)
                    state_update(SENT, c, 1, st["xd"])
                    if last and si > 0:
                        P.dma("sp", OUT["ossd"].v(OUT["ossd"].t[(j * 2 + si - 1) * 2 + 1, :, g * 512:(g + 1) * 512]), SENT[:])

                def recur_b1(k, st):
                    yy = YY[st["par"]]
                    P.tt("dve", yy[:], YO[0][:], YO[1][:], ALU.add)
                    P.tt("dve", yy[:], yy[:], PS[4][:], ALU.add)

                def recur_b2(k, st):
                    c, par = st["c"], st["par"]
                    yy, zt, xs2 = YY[par], ZT[par], st["xs2"]
                    P.tt("dve", yy[:], yy[:], xs2[:], ALU.add)
                    P.act(zt[:], zt[:], AF.Silu)
                    P.tt(EW2, yy[:], yy[:], zt[:], ALU.mult)
                    ss = SS[par]
                    P.memset("dve", ss[:], 0.0)
                    P.act(JK[:], yy[:], AF.Square, accum_out=ss[:])
                    P.act(ss[:], ss[:], AF.Ln, bias=EPS, scale=1.0 / 512)
                    P.act(ss[:], ss[:], AF.Exp, scale=-0.5)
                    yn = YN[par]
                    P.stt("dve", yn[:], yy[:], ss[:], NW[:], ALU.mult, ALU.mult)
                    pst = PS[7]
                    pb = pst.t[:].bitcast(BF16)
                    for i4 in range(4):
                        P.transpose(pst.v(pb[:, i4 * 128:(i4 + 1) * 128]), yn.v(yn.t[:, i4 * 128:(i4 + 1) * 128]), IDB)
                    evac(YGT.v(YGT.t[:, :, c * 128:(c + 1) * 128]), pst.v(pb[:, 0:512].rearrange("p (a b) -> p a b", a=4)))

                loadz(0)
                nst = common(0)
                indep_d(nst, 0)
                indep_d(nst, 1)
                ymm(nst)
                for k in range(len(order)):
                    st = nst
                    more = k + 1 < len(order)
                    if more:
                        loadz(k + 1)
                        nst = common(k + 1)
                        indep_d(nst, 0)
                    recur_a(k, st)
                    if more:
                        indep_d(nst, 1)
                    recur_b1(k, st)
                    if more:
                        ymm(nst)
                    recur_b2(k, st)
                P.dma("sp", MIX.v(MIX.t[g * 4:(g + 1) * 4].rearrange("c p t -> p c t"), [g * 4 + q for q in range(4)]), YGT[:])
        P.barrier()

    if ssd_dbg is not None:
        ssd_phase(0)
        P.assemble()
        es.close()
        return nc
    ada_phase()
    setup_small()
    res_in = IN["xT"]
    res_in_tok = Tile(IN["xT"].t, 32)
    cur = res_in_tok
    for i in range(n_layers):
        cur_layer[0] = i
        j = i // 2
        last = (i == n_layers - 1)
        hs = ExitStack()
        H = P.sb(hs, "H", [128, 16, T], BF16, ntok=NTILE)
        norm_phase(cur, i, 0, H)
        if i % 2 == 0:
            ef, ev = IN["ewin_fm"], IN["ewin_v"]
            proj_phase(H, [ef.v(ef.t[j * 32 + k]) for k in range(32)], 0,
                       [(ev.v(ev.t[j * 2 + k]), 512, k * 512, k) for k in range(2)])
            hs.close()
            if stop_after == ("proj", i):
                break
            lru_phase(j)
            att_phase(j, i)
            if stop_after == ("mix", i):
                break
            out_phase(16, 16, (MIX, IN["ewout"], j * 16), cur, XRES, i, 2)
        else:
            sf, sz, sd = IN["swin_fm"], IN["swin_z"], IN["swin_dt"]
            proj_phase(H, [sf.v(sf.t[j * 48 + k]) for k in range(48)], 0,
                       [(sz.v(sz.t[j * 8 + k]), 512, k * 512, k) for k in range(8)] + [(sd.v(sd.t[j]), 128, 4096, 8)])
            hs.close()
            if stop_after == ("proj", i):
                break
            ssd_phase(j)
            if stop_after == ("mix", i):
                break
            out_phase(32, 32, (MIX, IN["swout"], j * 16), cur, XRES, i, 2)
        cur = XRES
        if stop_after == ("mixout", i):
            break
        hs = ExitStack()
        H = P.sb(hs, "H", [128, 16, T], BF16, ntok=NTILE)
        norm_phase(XRES, i, 1, H)
        ffn_in_phase(H, i)
        hs.close()
        dst = Tile(OUT["yT"].t, 32) if last else XRES
        out_phase(FFC, FFC, (FF, IN["fwout"], i * 16), XRES, dst, i, 5)
    P.assemble()
    es.close()
    return nc


_CACHE = {}


def run_cores(inp, cores, n_layers=DEPTH, debug=False, stop_after=None, trace=False):
    key = (n_layers, debug, stop_after)
    if key not in _CACHE:
        _CACHE[key] = build_program(n_layers, debug, stop_after)
    nc = _CACHE[key]
    sh = shared_inputs(inp)
    in_maps = []
    for b in cores:
        m = dict(sh)
        m.update(core_inputs(inp, b))
        in_maps.append(m)
    res = run_bass_kernel_spmd(nc, in_maps, core_ids=list(range(len(cores))), trace=trace)
    return res


def assemble_outputs(results):
    yp = np.zeros((16, P_LEN, D), np.float32)
    ys = np.zeros((8, S_LEN, D), np.float32)
    nk = np.zeros((16, 2, P_LEN, 8, 128), np.float32)
    nv = np.zeros((16, 2, P_LEN, 8, 128), np.float32)
    nl = np.zeros((16, 2, 2, D_RNN), np.float32)
    nss = np.zeros((16, 2, 2, 64, 64, 128), np.float32)
    for b, r in enumerate(results):
        y = np.asarray(r["yT"]).reshape(D, T).T
        ys[b] = y[:S_LEN]
        yp[2 * b] = y[S_LEN:S_LEN + P_LEN]
        yp[2 * b + 1] = y[S_LEN + P_LEN:]
        ok = np.asarray(r["ok"]).reshape(2, 2, P_LEN, 8, 128)
        ov = np.asarray(r["ov"]).reshape(2, 2, P_LEN, 8, 128)
        ol = np.asarray(r["olru"]).reshape(2, 128, 8, 2, 2)
        os_ = np.asarray(r["ossd"]).reshape(2, 2, 2, 128, 64, 64)
        for pi in range(2):
            nk[2 * b + pi] = ok[:, pi]
            nv[2 * b + pi] = ov[:, pi]
            nl[2 * b + pi] = ol[:, :, :, pi, :].transpose(0, 3, 2, 1).reshape(2, 2, D_RNN)
            nss[2 * b + pi] = os_[:, pi].transpose(0, 1, 3, 4, 2)
    return yp, ys, nk, nv, nl, nss


def kernel(**inputs):
    res = run_cores(inputs, list(range(NCORES)))
    return assemble_outputs(res.results)
```

```python
import math
from contextlib import ExitStack

import numpy as np
import concourse.bass as bass
import concourse.mybir as mybir
from concourse.bass_utils import run_bass_kernel_spmd

F32 = mybir.dt.float32
BF16 = mybir.dt.bfloat16
AF = mybir.ActivationFunctionType
ALU = mybir.AluOpType

D = 2048
KC = 16
DEPTH = 4
NCORES = 8
S_LEN = 2048
P_LEN = 256
T = S_LEN + 2 * P_LEN
NTILE = T // 512
NBLK = T // 128
EPS = 1e-6
D_RNN = 1024
D_FF = 5632
FFC = D_FF // 128
D_INNER = 4096
SEGS = [(0, S_LEN), (S_LEN, S_LEN + P_LEN), (S_LEN + P_LEN, T)]
GRID_W = 64

C_ID, C_ROT, C_MF, C_MB, C_SL, C_SU, C_ONE, C_B64 = range(8)
NCONSTF = 8


class Tok:
    __slots__ = ("w", "r")

    def __init__(self):
        self.w = None
        self.r = {}


class View:
    __slots__ = ("ap", "toks")

    def __init__(self, ap, toks):
        self.ap = ap
        self.toks = toks


class Tile:
    def __init__(self, t, ntok=1):
        self.t = t
        self.toks = [Tok() for _ in range(ntok)]

    def __getitem__(self, idx):
        return View(self.t[idx], self.toks)

    def v(self, ap, ti=None):
        if ti is None:
            toks = self.toks
        elif isinstance(ti, int):
            toks = [self.toks[ti]]
        else:
            toks = [self.toks[i] for i in ti]
        return View(ap, toks)


ENGS = ("pe", "act", "dve", "pool", "sp")
import os
SAME_ENGINE_SYNC = os.environ.get("NOSES") is None
PROFILE_SCOPES = False
ADA_OVERLAP = False


class Prog:
    def __init__(self, nc, es):
        self.nc = nc
        self.es = es
        self.ops = {e: [] for e in ENGS}
        self.count = {e: 0 for e in ENGS}
        self.known = {e: {} for e in ENGS}
        self.esem = {e: es.enter_context(nc.semaphore("sem_" + e)) for e in ENGS}
        self.sems = dict(self.esem)
        self.dma_pool = {}
        self.dma_rr = {}
        self.dma_val = {}
        for q, n in (("sp", 40), ("pool", 24)):
            names = []
            for i in range(n):
                nm = "dq_%s_%d" % (q, i)
                self.sems[nm] = es.enter_context(nc.semaphore(nm))
                self.dma_val[nm] = 0
                names.append(nm)
            self.dma_pool[q] = names
            self.dma_rr[q] = 0
        self.uid = 0
        self.scope = "setup"

    def name(self, base):
        self.uid += 1
        return "%s_%d" % (base, self.uid)

    def sb(self, es, base, shape, dtype, ntok=1):
        return Tile(es.enter_context(self.nc.sbuf_tensor(self.name(base), list(shape), dtype)), ntok)

    def _deps(self, reads, writes):
        deps = {}

        def add(s, v):
            if deps.get(s, 0) < v:
                deps[s] = v

        for vw in reads:
            for t in vw.toks:
                if t.w is not None:
                    add(*t.w)
        for vw in writes:
            for t in vw.toks:
                if t.w is not None:
                    add(*t.w)
                for s, v in t.r.items():
                    add(s, v)
        return deps

    def _waits(self, eng, deps):
        waits = []
        kn = self.known[eng]
        for s, v in deps.items():
            if s == eng and (eng == "pe" or not SAME_ENGINE_SYNC):
                continue
            if kn.get(s, 0) >= v:
                continue
            kn[s] = v
            waits.append((s, v))
        return waits

    def op(self, eng, fn, reads=(), writes=()):
        deps = self._deps(reads, writes)
        waits = self._waits(eng, deps)
        self.count[eng] += 1
        k = self.count[eng]
        self.ops[eng].append((waits, fn, (eng, 1), self.scope))
        for vw in reads:
            for t in vw.toks:
                if t.r.get(eng, 0) < k:
                    t.r[eng] = k
        for vw in writes:
            for t in vw.toks:
                t.w = (eng, k)
                t.r = {}

    def dma(self, q, out, in_, **kw):
        pool = self.dma_pool[q]
        s = pool[self.dma_rr[q] % len(pool)]
        self.dma_rr[q] += 1
        deps = self._deps([in_], [out])
        prev = self.dma_val[s]
        if prev > 0 and deps.get(s, 0) < prev:
            deps[s] = prev
        waits = self._waits(q, deps)
        target = prev + 16
        self.dma_val[s] = target
        oap, iap = out.ap, in_.ap

        def fn(e):
            return e.dma_start(out=oap, in_=iap, **kw)

        self.ops[q].append((waits, fn, (s, 16), self.scope))
        for t in in_.toks:
            if t.r.get(s, 0) < target:
                t.r[s] = target
        for t in out.toks:
            t.w = (s, target)
            t.r = {}

    def barrier(self):
        for e in ENGS:
            deps = {}
            for e2 in ENGS:
                if e2 != e and self.count[e2] > 0:
                    deps[e2] = self.count[e2]
            if e != "pe" and self.count[e] > 0:
                deps[e] = self.count[e]
            for s, v in self.dma_val.items():
                if v > 0:
                    deps[s] = v
            waits = []
            kn = self.known[e]
            for s, v in deps.items():
                if kn.get(s, 0) >= v:
                    continue
                kn[s] = v
                waits.append((s, v))
            if waits:
                self.ops[e].append((waits, None, None, self.scope))

    def mm(self, out, lhsT, rhs, start=True, stop=True, sgc=False):
        o, l, r = out.ap, lhsT.ap, rhs.ap
        if sgc:
            self.op("pe", lambda e: e.matmul(o, l, r, start=start, stop=stop, skip_group_check=True), [lhsT, rhs], [out])
        else:
            self.op("pe", lambda e: e.matmul(o, l, r, start=start, stop=stop), [lhsT, rhs], [out])

    def transpose(self, out, in_, ident):
        o, i, d = out.ap, in_.ap, ident.ap
        self.op("pe", lambda e: e.transpose(out=o, in_=i, identity=d), [in_, ident], [out])

    def act(self, out, in_, func, bias=None, scale=None, accum_out=None, eng="act"):
        kw = {}
        reads = [in_]
        writes = [out]
        if bias is not None:
            if isinstance(bias, View):
                kw["bias"] = bias.ap
                reads.append(bias)
            else:
                kw["bias"] = float(bias)
        if scale is not None:
            if isinstance(scale, View):
                kw["scale"] = scale.ap
                reads.append(scale)
            else:
                kw["scale"] = float(scale)
        if accum_out is not None:
            kw["accum_out"] = accum_out.ap
            writes.append(accum_out)
        o, i = out.ap, in_.ap
        self.op("act", lambda e: e.activation(out=o, in_=i, func=func, **kw), reads, writes)

    def tt(self, eng, out, in0, in1, op):
        o, a, b = out.ap, in0.ap, in1.ap
        self.op(eng, lambda e: e.tensor_tensor(out=o, in0=a, in1=b, op=op), [in0, in1], [out])

    def ts(self, eng, out, in0, s1, s2, op0, op1=None):
        reads = [in0]
        a1 = s1
        a2 = s2
        if isinstance(s1, View):
            reads.append(s1)
            a1 = s1.ap
        if isinstance(s2, View):
            reads.append(s2)
            a2 = s2.ap
        o, a = out.ap, in0.ap
        if op1 is None:
            self.op(eng, lambda e: e.tensor_scalar(out=o, in0=a, scalar1=a1, scalar2=None, op0=op0), reads, [out])
        else:
            self.op(eng, lambda e: e.tensor_scalar(out=o, in0=a, scalar1=a1, scalar2=a2, op0=op0, op1=op1),
                    reads, [out])

    def stt(self, eng, out, in0, scalar, in1, op0, op1):
        reads = [in0, in1]
        sc = scalar
        if isinstance(scalar, View):
            reads.append(scalar)
            sc = scalar.ap
        o, a, b = out.ap, in0.ap, in1.ap
        self.op(eng, lambda e: e.scalar_tensor_tensor(out=o, in0=a, scalar=sc, in1=b, op0=op0, op1=op1),
                reads, [out])

    def copy(self, eng, out, in_):
        o, i = out.ap, in_.ap
        if eng == "act":
            self.op("act", lambda e: e.copy(out=o, in_=i), [in_], [out])
        else:
            self.op(eng, lambda e: e.tensor_copy(out=o, in_=i), [in_], [out])

    def recip(self, out, in_):
        o, i = out.ap, in_.ap
        self.op("dve", lambda e: e.reciprocal(out=o, in_=i), [in_], [out])

    def memset(self, eng, out, val):
        o = out.ap
        self.op(eng, lambda e: e.memset(o, val), [], [out])

    def scan(self, out, d0, d1, initial):
        reads = [d0, d1]
        ini = initial
        if isinstance(initial, View):
            reads.append(initial)
            ini = initial.ap
        o, a, b = out.ap, d0.ap, d1.ap
        self.op("dve", lambda e: e.tensor_tensor_scan(out=o, data0=a, data1=b, initial=ini,
                                                       op0=ALU.mult, op1=ALU.add), reads, [out])

    def assemble(self):
        self.barrier()
        nc = self.nc
        block = self.es.enter_context(nc.Block())
        sems = self.sems

        def run(e, ops):
            i = 0
            n = len(ops)
            while i < n:
                sc = ops[i][3]
                k = i
                while k < n and ops[k][3] == sc:
                    k += 1
                if PROFILE_SCOPES:
                    cm = nc.named_scope(sc)
                else:
                    cm = ExitStack()
                with cm:
                    for waits, fn, inc, _ in ops[i:k]:
                        for s, v in waits:
                            e.wait_ge(sems[s], v)
                        if fn is not None:
                            ins = fn(e)
                            ins.then_inc(sems[inc[0]], inc[1])
                i = k

        block.tensor(lambda e: run(e, self.ops["pe"]))
        block.scalar(lambda e: run(e, self.ops["act"]))
        block.vector(lambda e: run(e, self.ops["dve"]))
        block.gpsimd(lambda e: run(e, self.ops["pool"]))
        block.sync(lambda e: run(e, self.ops["sp"]))


def bc(ap, axis, shape):
    return ap.unsqueeze(axis).broadcast_to(list(shape))


def pf_layout():
    off = {}
    n = 0

    def add(name, cols):
        nonlocal n
        off[name] = n
        n += cols

    add("cond", 32)
    for i in range(DEPTH):
        add("bada%d" % i, 96)
    for i in range(DEPTH):
        add("g%d" % i, 32)
    for j in range(2):
        add("lcw%d" % j, 32)
        add("lcb%d" % j, 8)
        add("lbr%d" % j, 16)
        add("lbi%d" % j, 16)
        add("llam%d" % j, 16)
        add("lh0%d" % j, 16)
        add("qn%d" % j, 1)
        add("kn%d" % j, 1)
        add("sub%d" % j, 1)
    for j in range(2):
        add("scw%d" % j, 192)
        add("scb%d" % j, 48)
    return off, n


def pr_layout():
    off = {}
    n = 0
    for j in range(2):
        off["subln%d" % j] = n
        n += 128
    for j in range(2):
        off["dtb%d" % j] = n
        n += 128
        off["alog%d" % j] = n
        n += 128
        off["dsk%d" % j] = n
        n += 64
    return off, n


PF_OFF, NPF = pf_layout()
PR_OFF, NPR = pr_layout()


def fm(v):
    v = np.asarray(v, np.float32).reshape(-1, 128)
    return np.ascontiguousarray(v.T)


def blk_fm(W, c0, c1, bw=128):
    K = W.shape[0]
    Wc = W[:, c0:c1]
    nb = (c1 - c0) // bw
    return np.ascontiguousarray(Wc.reshape(K // 128, 128, nb, bw).transpose(2, 1, 0, 3))


def shared_inputs(inp):
    sh = {}
    wa = np.asarray(inp["w_ada"], np.float32).reshape(DEPTH, 16, 128, 24, 4, 128)
    sh["wada"] = np.ascontiguousarray(wa.transpose(0, 3, 2, 4, 1, 5)).reshape(DEPTH * 24, 128, 4 * 16 * 128)
    ewi = np.asarray(inp["even_w_in"], np.float32)
    sh["ewin_fm"] = np.stack([blk_fm(ewi[j], 0, 4096) for j in range(2)]).reshape(2 * 32, 128, 16 * 128)
    sh["ewin_v"] = np.stack([blk_fm(ewi[j], 4096, 5120, 512) for j in range(2)]).reshape(2 * 2, 128, 16 * 512)
    ewo = np.asarray(inp["even_w_out"], np.float32)
    sh["ewout"] = np.stack([blk_fm(ewo[j], 0, 2048) for j in range(2)]).reshape(2 * 16, 128, 16 * 128)
    swi = np.asarray(inp["ssd_w_in"], np.float32)
    sh["swin_fm"] = np.stack([blk_fm(swi[j], 4096, 10240) for j in range(2)]).reshape(2 * 48, 128, 16 * 128)
    sh["swin_z"] = np.stack([blk_fm(swi[j], 0, 4096, 512) for j in range(2)]).reshape(2 * 8, 128, 16 * 512)
    sh["swin_dt"] = np.stack([blk_fm(swi[j], 10240, 10368) for j in range(2)]).reshape(2, 128, 16 * 128)
    swo = np.asarray(inp["ssd_w_out"], np.float32)
    sh["swout"] = np.stack([blk_fm(swo[j], 0, 2048) for j in range(2)]).reshape(2 * 16, 128, 32 * 128)
    fwi = np.asarray(inp["ffn_w_in"], np.float32)
    sh["fwin"] = np.stack([blk_fm(fwi[i], 0, 2 * D_FF) for i in range(DEPTH)]).reshape(DEPTH * 88, 128, 16 * 128)
    fwo = np.asarray(inp["ffn_w_out"], np.float32)
    sh["fwout"] = np.stack([blk_fm(fwo[i], 0, 2048) for i in range(DEPTH)]).reshape(DEPTH * 16, 128, 44 * 128)
    sh["lwr"] = np.ascontiguousarray(np.asarray(inp["lru_w_r"], np.float32).transpose(0, 3, 1, 2, 4)).reshape(2, 128, 2 * 8 * 128)
    sh["lwi"] = np.ascontiguousarray(np.asarray(inp["lru_w_i"], np.float32).transpose(0, 3, 1, 2, 4)).reshape(2, 128, 2 * 8 * 128)
    pr = np.zeros((NPR,), np.float32)
    for j in range(2):
        pr[PR_OFF["subln%d" % j]:][:128] = np.asarray(inp["da_subln"])[j]
        pr[PR_OFF["dtb%d" % j]:][:128] = np.asarray(inp["ssd_dt_bias"])[j].reshape(-1)
        pr[PR_OFF["alog%d" % j]:][:128] = np.asarray(inp["ssd_a_log"])[j].reshape(-1)
        pr[PR_OFF["dsk%d" % j]:][:64] = np.asarray(inp["ssd_d"])[j]
    sh["prow"] = np.ascontiguousarray(np.broadcast_to(pr[None, :], (128, NPR)))
    sh["normw"] = np.ascontiguousarray(np.broadcast_to(np.asarray(inp["ssd_norm_w"], np.float32)[:, None, :], (2, 128, D_INNER)))
    sh["dalam"] = np.asarray(inp["da_lambda"], np.float32).reshape(1, 512)
    cf = np.zeros((128, NCONSTF, 128), np.float32)
    ii = np.arange(128)
    cf[:, C_ID, :] = np.eye(128)
    rot = np.zeros((128, 128), np.float32)
    for m in range(128):
        q = (m % 64) // 16
        if q % 2 == 0:
            rot[m + 16, m] = -1.0
        else:
            rot[m - 16, m] = 1.0
    cf[:, C_ROT, :] = rot
    cf[:, C_MF, :] = (ii[:, None] <= ii[None, :])
    cf[:, C_MB, :] = (ii[:, None] >= ii[None, :])
    cf[:, C_SL, :] = (ii[:, None] > ii[None, :])
    cf[:, C_SU, :] = (ii[:, None] < ii[None, :])
    cf[:, C_ONE, :] = 1.0
    b64 = np.zeros((128, 128), np.float32)
    b64[:64, :64] = 1.0 / 64
    b64[64:, 64:] = 1.0 / 64
    cf[:, C_B64, :] = b64
    sh["constf"] = cf.reshape(128, NCONSTF * 128)
    pos = np.arange(S_LEN)
    row = (pos // GRID_W).astype(np.float32)
    col = (pos % GRID_W).astype(np.float32)
    inv = (10000.0 ** (-np.arange(16, dtype=np.float32) / 16)).astype(np.float32)
    ang = np.concatenate([row[:, None] * inv, row[:, None] * inv, col[:, None] * inv, col[:, None] * inv], -1)
    ang2 = np.concatenate([ang, ang], -1).T.astype(np.float32)
    sh["ropecs"] = np.ascontiguousarray(np.stack([np.cos(ang2), np.sin(ang2)], 1).astype(np.float32)).reshape(128, 2 * S_LEN)
    return sh


def core_inputs(inp, b):
    ci = {}
    xs = np.asarray(inp["x_sample"], np.float32)[b]
    xp = np.asarray(inp["x_prompt"], np.float32)
    X = np.concatenate([xs, xp[2 * b], xp[2 * b + 1]], 0)
    ci["xT"] = np.ascontiguousarray(X.T).reshape(16, 128, T)
    pf = np.zeros((128, NPF), np.float32)
    o = PF_OFF["cond"]
    pf[:, o:o + 32].reshape(128, 16, 2)[:, :, 0] = fm(np.asarray(inp["c"])[b])
    pf[:, o:o + 32].reshape(128, 16, 2)[:, :, 1] = fm(np.asarray(inp["c_ctx"]))
    for i in range(DEPTH):
        o = PF_OFF["bada%d" % i]
        pf[:, o:o + 96] = fm(np.asarray(inp["b_ada"])[i])
        o = PF_OFF["g%d" % i]
        pf[:, o:o + 32] = fm(np.asarray(inp["norm_g"])[i].reshape(-1))
    for j in range(2):
        o = PF_OFF["lcw%d" % j]
        pf[:, o:o + 32] = fm(np.asarray(inp["lru_conv_w"])[j].reshape(-1))
        o = PF_OFF["lcb%d" % j]
        pf[:, o:o + 8] = fm(np.asarray(inp["lru_conv_b"])[j])
        for nm, key in (("lbr", "lru_b_r"), ("lbi", "lru_b_i"), ("llam", "lru_lambda")):
            o = PF_OFF["%s%d" % (nm, j)]
            pf[:, o:o + 16] = fm(np.asarray(inp[key])[j].reshape(-1))
        o = PF_OFF["lh0%d" % j]
        pf[:, o:o + 16] = fm(np.asarray(inp["state_lru"])[b, j].reshape(-1))
        pf[:, PF_OFF["qn%d" % j]] = np.asarray(inp["da_q_norm"])[j].reshape(-1)
        pf[:, PF_OFF["kn%d" % j]] = np.asarray(inp["da_k_norm"])[j].reshape(-1)
        pf[:, PF_OFF["sub%d" % j]] = np.asarray(inp["da_subln"])[j]
        o = PF_OFF["scw%d" % j]
        pf[:, o:o + 192] = fm(np.asarray(inp["ssd_conv_w"])[j].reshape(-1))
        o = PF_OFF["scb%d" % j]
        pf[:, o:o + 48] = fm(np.asarray(inp["ssd_conv_b"])[j])
    ci["pfm"] = pf
    ck = np.asarray(inp["cache_attn_k"], np.float32)[b]
    ci["kcT"] = np.ascontiguousarray(ck.transpose(0, 2, 3, 1)).reshape(16, 128, 256)
    ci["vc"] = np.ascontiguousarray(np.asarray(inp["cache_attn_v"], np.float32)[b]).reshape(2, 256, 1024)
    ss = np.asarray(inp["state_ssd"], np.float32)[b]
    ci["ssd0T"] = np.ascontiguousarray(ss.transpose(0, 1, 4, 2, 3)).reshape(4, 128, 4096)
    return ci


IN_SHAPES = {
    "xT": [16, 128, T], "pfm": [128, NPF], "kcT": [16, 128, 256], "vc": [2, 256, 1024], "ssd0T": [4, 128, 4096],
    "wada": [DEPTH * 24, 128, 8192], "ewin_fm": [64, 128, 2048], "ewin_v": [4, 128, 8192], "ewout": [32, 128, 2048],
    "swin_fm": [96, 128, 2048], "swin_z": [16, 128, 8192], "swin_dt": [2, 128, 2048], "swout": [32, 128, 4096],
    "fwin": [DEPTH * 88, 128, 2048], "fwout": [DEPTH * 16, 128, 5632], "lwr": [2, 128, 2048], "lwi": [2, 128, 2048],
    "prow": [128, NPR], "normw": [2, 128, D_INNER], "dalam": [1, 512], "constf": [128, NCONSTF * 128],
    "ropecs": [128, 2 * S_LEN],
}
OUT_SHAPES = {
    "yT": [16, 128, T], "ok": [2, 2, 256, 1024], "ov": [2, 2, 256, 1024], "olru": [2, 128, 32],
    "ossd": [8, 128, 4096],
}


def build_program(n_layers=DEPTH, debug=False, stop_after=None):
    nc = bass.Bass("TRN2", target_bir_lowering=False)
    es = ExitStack()
    P = Prog(nc, es)
    IN = {k: Tile(nc.dram_tensor(k, v, F32, kind="ExternalInput").ap()) for k, v in IN_SHAPES.items()}
    OUT = {k: Tile(nc.dram_tensor(k, v, F32, kind="ExternalOutput").ap()) for k, v in OUT_SHAPES.items()}
    skind = "ExternalOutput" if debug else "Internal"
    XRES = Tile(nc.dram_tensor("XRES", [16, 128, T], F32, kind=skind).ap(), 32)
    UFM = Tile(nc.dram_tensor("UFM", [48, 128, T], F32, kind=skind).ap(), 48)
    UTM = Tile(nc.dram_tensor("UTM", [T, 4224], F32, kind=skind).ap(), 9)
    MIX = Tile(nc.dram_tensor("MIX", [32, 128, T], BF16, kind=skind).ap(), 32)
    FF = Tile(nc.dram_tensor("FF", [FFC, 128, T], BF16, kind=skind).ap(), FFC)

    PF = P.sb(es, "PF", [128, NPF], F32)
    PR = P.sb(es, "PR", [128, NPR], F32)
    CF = P.sb(es, "CF", [128, NCONSTF, 128], F32)
    CB = P.sb(es, "CB", [128, 7, 128], BF16)
    ADA = P.sb(es, "ADA", [128, DEPTH, 96, 2], F32)
    MODA = P.sb(es, "MODA", [128, DEPTH, 2, 2, 16], F32)
    CNEG = P.sb(es, "CNEG", [128, 2, 16], F32)
    NLAM = P.sb(es, "NLAM", [128, 2], F32)
    SILC = P.sb(es, "SILC", [128, 16, 2], BF16)
    DAL = P.sb(es, "DAL", [1, 512], F32)
    DAL2 = P.sb(es, "DAL2", [1, 8], F32)
    PS = [Tile(es.enter_context(nc.psum_tensor("ps%d" % i, [128, 512], F32))) for i in range(8)]
    psrr = [0]

    def nextps(lo=0, hi=8):
        p = PS[lo + psrr[0] % (hi - lo)]
        psrr[0] += 1
        return p

    evrr = [0]

    def evac(out, in_):
        eng = "act" if evrr[0] % 2 == 0 else "dve"
        evrr[0] += 1
        P.copy(eng, out, in_)

    def pfc(name, a, b=None):
        o = PF_OFF[name]
        if b is None:
            return PF.v(PF.t[:, o + a:o + a + 1])
        return PF.v(PF.t[:, o + a:o + b])

    def prc(name, a, b):
        o = PR_OFF[name]
        return PR.v(PR.t[:, o + a:o + b])

    def cfm(idx):
        return CF.v(CF.t[:, idx, :])

    IDB = CB.v(CB.t[:, 0, :])
    ONESB = CB.v(CB.t[:, 1, :])
    B64B = CB.v(CB.t[:, 2, :])

    P.dma("sp", PF[:], IN["pfm"][:])
    P.dma("sp", PR[:], IN["prow"][:])
    P.dma("sp", CF.v(CF.t[:].rearrange("p a b -> p (a b)")), IN["constf"][:])
    P.dma("sp", DAL[:], IN["dalam"][:])
    P.copy("dve", IDB, cfm(C_ID))
    P.memset("dve", ONESB, 1.0)
    P.copy("dve", B64B, cfm(C_B64))
    P.copy("dve", CB.v(CB.t[:, 3, :]), cfm(C_SL))
    P.copy("dve", CB.v(CB.t[:, 4, :]), cfm(C_SU))
    P.copy("dve", CB.v(CB.t[:, 5, :]), cfm(C_MF))
    P.copy("dve", CB.v(CB.t[:, 6, :]), cfm(C_MB))

    def rsqrt_inplace(v, scale, bias=EPS):
        P.act(v, v, AF.Sqrt, bias=bias, scale=scale)
        P.recip(v, v)

    ada_n = [0]

    def ada_unit(i, bg, WS, ps):
        w = WS[ada_n[0] % len(WS)]
        ada_n[0] += 1
        P.dma("pool", w.v(w.t[:].rearrange("p a k j -> p (a k j)")), IN["wada"].v(IN["wada"].t[i * 24 + bg]))
        for blk in range(4):
            jc = (bg * 4 + blk) * 2
            for kc in range(16):
                P.mm(ps.v(ps.t[:, jc:jc + 2]), w.v(w.t[:, blk, kc, :]), SILC.v(SILC.t[:, kc, :]),
                     start=(kc == 0), stop=(kc == 15))

    def ada_finish(i, ps):
        psv = ps.t[:, 0:192].rearrange("p (j c) -> p j c", c=2)
        for c in range(2):
            P.tt("dve", ADA.v(ADA.t[:, i, :, c]), ps.v(psv[:, :, c]), pfc("bada%d" % i, 0, 96), ALU.add)
        for s_ in range(2):
            for c in range(2):
                P.stt("dve", MODA.v(MODA.t[:, i, s_, c, :]), ADA.v(ADA.t[:, i, (3 * s_ + 1) * 16:(3 * s_ + 2) * 16, c]),
                      1.0, pfc("g%d" % i, s_ * 16, (s_ + 1) * 16), ALU.add, ALU.mult)

    def ada_phase():
        P.scope = "ada"
        with ExitStack() as ph:
            WS = [P.sb(ph, "wada", [128, 4, 16, 128], BF16) for _ in range(3)]
            P.act(SILC.v(SILC.t[:].rearrange("p k c -> p (k c)")), pfc("cond", 0, 32), AF.Silu)
            for li in range(1 if ADA_OVERLAP else n_layers):
                for bg in range(24):
                    ada_unit(li, bg, WS, PS[li % 8])
                ada_finish(li, PS[li % 8])
        P.barrier()

    def ada_col(i, which, kc, c):
        return ADA.v(ADA.t[:, i, which * 16 + kc:which * 16 + kc + 1, c])

    def norm_phase(src, i, s, H):
        P.scope = "L%d_norm%d" % (i, s)
        with ExitStack() as ph:
            XT = [P.sb(ph, "nx", [128, 16, 512], F32, ntok=4) for _ in range(2)]
            SQ = [P.sb(ph, "nsq", [128, 16, 512], BF16) for _ in range(2)]
            RS = [P.sb(ph, "nrs", [128, 512], F32) for _ in range(2)]
            TM = [P.sb(ph, "ntm", [128, 512], F32) for _ in range(3)]
            srcv = src.t.rearrange("c p t -> p c t")

            def load(t):
                x = XT[t % 2]
                for q in range(4):
                    P.dma("sp", x.v(x.t[:, q * 4:(q + 1) * 4, :], q), src.v(srcv[:, q * 4:(q + 1) * 4, t * 512:(t + 1) * 512]))

            load(0)
            for t in range(NTILE):
                if t + 1 < NTILE:
                    load(t + 1)
                c = 0 if t < 4 else 1
                x, sq, rs = XT[t % 2], SQ[t % 2], RS[t % 2]
                for q in range(4):
                    P.act(sq.v(sq.t[:, q * 4:(q + 1) * 4, :]), x.v(x.t[:, q * 4:(q + 1) * 4, :], q), AF.Square)
                ps = nextps()
                for kc in range(16):
                    P.mm(ps[:], ONESB, sq.v(sq.t[:, kc, :]), start=(kc == 0), stop=(kc == 15))
                P.act(rs[:], ps[:], AF.Ln, bias=EPS, scale=1.0 / D)
                P.act(rs[:], rs[:], AF.Exp, scale=-0.5)
                for kc in range(16):
                    tm = TM[kc % 3]
                    P.tt("dve", tm[:], x.v(x.t[:, kc, :], kc // 4), rs[:], ALU.mult)
                    P.act(H.v(H.t[:, kc, t * 512:(t + 1) * 512], t), tm[:], AF.Identity,
                          bias=ada_col(i, 3 * s, kc, c), scale=MODA.v(MODA.t[:, i, s, c, kc:kc + 1]))
        P.barrier()

    def gemm_fm(R, kcn, tiles, groups, wslots, epilogue, rtok=None, psr=(0, 8)):
        if rtok is None:
            rtok = lambda kc, tk: tk
        nws = 0
        for gi, grp in enumerate(groups):
            ws = []
            for wv in grp:
                w = wslots[nws % len(wslots)]
                nws += 1
                P.dma("pool", w.v(w.t[:].rearrange("p k j -> p (k j)")), wv)
                ws.append(w)
            for ti, (off, n, tk) in enumerate(tiles):
                pss = []
                for w in ws:
                    ps = nextps(*psr)
                    for kc in range(kcn):
                        P.mm(ps.v(ps.t[:, 0:n]), w.v(w.t[:, kc, :]), R.v(R.t[:, kc, off:off + n], rtok(kc, tk)),
                             start=(kc == 0), stop=(kc == kcn - 1))
                    pss.append(ps)
                epilogue(gi, ti, pss)

    def gemm_tm(R, kcn, groups, wslots, epilogue):
        for gi, (wv, bw) in enumerate(groups):
            w = wslots[gi % len(wslots)]
            P.dma("pool", w.v(w.t[:, :, 0:bw]), View(wv.ap.rearrange("p (k j) -> p k j", j=bw), wv.toks))
            for tb in range(NBLK):
                ps = nextps()
                for kc in range(kcn):
                    P.mm(ps.v(ps.t[:, 0:bw]), R.v(R.t[:, kc, tb * 128:(tb + 1) * 128], tb // 4), w.v(w.t[:, kc, 0:bw]),
                         start=(kc == 0), stop=(kc == kcn - 1))
                epilogue(gi, tb, ps)

    TILES5 = [(t * 512, 512, t) for t in range(NTILE)]
    cur_layer = [0]

    def proj_phase(H, fm_w, fm_chunk0, tm_list):
        P.scope = "L%d_proj" % cur_layer[0]
        with ExitStack() as ph:
            WS = [P.sb(ph, "wfm", [128, 16, 128], BF16) for _ in range(4)]
            ROW = [P.sb(ph, "urow", [128, T], F32) for _ in range(2)]

            def epi(gi, ti, pss):
                row = ROW[gi % 2]
                evac(row.v(row.t[:, ti * 512:(ti + 1) * 512]), pss[0][:])
                if ti == NTILE - 1:
                    ch = fm_chunk0 + gi
                    P.dma("sp", UFM.v(UFM.t[ch], ch), row[:])

            gemm_fm(H, 16, TILES5, [[wv] for wv in fm_w], WS, epi)
            P.barrier()
        with ExitStack() as ph:
            WT = [P.sb(ph, "wtm", [128, 16, 512], BF16) for _ in range(2)]
            ST = [P.sb(ph, "ust", [128, 512], F32) for _ in range(4)]
            cnt = [0]

            def epi2(gi, tb, ps):
                wv, bw, col0, tk = tm_list[gi]
                st = ST[cnt[0] % 4]
                cnt[0] += 1
                evac(st.v(st.t[:, 0:bw]), ps.v(ps.t[:, 0:bw]))
                P.dma("sp", UTM.v(UTM.t[tb * 128:(tb + 1) * 128, col0:col0 + bw], tk), st.v(st.t[:, 0:bw]))

            gemm_tm(H, 16, [(wv, bw) for (wv, bw, col0, tk) in tm_list], WT, epi2)
        P.barrier()

    def out_phase(kcn, nchunks_in, wsrc, res_src, res_dst, i, which):
        src, w_in, wbase = wsrc
        P.scope = "L%d_out%d" % (i, which)
        groups_tok = [[(0, 512, 0), (512, 512, 0), (1024, 256, 0)],
                      [(0, 512, 0), (512, 256, 0), (768, 512, 1)]]
        with ExitStack() as ph:
            R = P.sb(ph, "outR", [128, kcn, 1280], BF16, ntok=kcn)
            WS = [P.sb(ph, "wout", [128, kcn, 128], BF16) for _ in range(3)]
            XO = [P.sb(ph, "xold", [128, 1280], F32) for _ in range(2)]
            XN = [P.sb(ph, "xnew", [128, 1280], F32) for _ in range(2)]
            for tg in range(2):
                t0 = tg * 1280
                for kc in range(kcn):
                    P.dma("sp", R.v(R.t[:, kc, :], kc), src.v(src.t[kc, :, t0:t0 + 1280], kc))
                tiles = [(off, n, 0) for (off, n, c) in groups_tok[tg]]

                def epi(gi, ti, pss, tg=tg, t0=t0):
                    off, n, c = groups_tok[tg][ti]
                    xo, xn = XO[gi % 2], XN[gi % 2]
                    if ti == 0:
                        P.dma("sp", xo[:], res_src.v(res_src.t[gi, :, t0:t0 + 1280], gi * 2 + tg))
                    P.stt("dve", xn.v(xn.t[:, off:off + n]), pss[0].v(pss[0].t[:, 0:n]), ada_col(i, which, gi, c),
                          xo.v(xo.t[:, off:off + n]), ALU.mult, ALU.add)
                    if ti == 2:
                        P.dma("sp", res_dst.v(res_dst.t[gi, :, t0:t0 + 1280], gi * 2 + tg), xn[:])

                gemm_fm(R, kcn, tiles, [[w_in.v(w_in.t[wbase + ob])] for ob in range(16)], WS, epi,
                        rtok=lambda kc, tk: kc)
        P.barrier()

    def ffn_in_phase(H, i):
        P.scope = "L%d_ffnin" % i
        with ExitStack() as ph:
            WS = [P.sb(ph, "wff", [128, 16, 128], BF16) for _ in range(6)]
            SG = [P.sb(ph, "sg", [128, 512], F32) for _ in range(3)]
            ROW = [P.sb(ph, "frow", [128, T], BF16) for _ in range(2)]
            fw = IN["fwin"]
            cnt = [0]
            do_ada = ADA_OVERLAP and (i + 1 < n_layers)
            AWS = [P.sb(ph, "wada2", [128, 4, 16, 128], BF16) for _ in range(3)] if do_ada else None

            def epi(gi, ti, pss):
                if do_ada and ti == 2:
                    if gi < 24:
                        P.scope = "ada"
                        ada_unit(i + 1, gi, AWS, PS[7])
                        P.scope = "L%d_ffnin" % i
                    elif gi == 24:
                        ada_finish(i + 1, PS[7])
                sg = SG[cnt[0] % 3]
                cnt[0] += 1
                row = ROW[gi % 2]
                P.act(sg[:], pss[0][:], AF.Silu)
                P.tt("dve", row.v(row.t[:, ti * 512:(ti + 1) * 512]), sg[:], pss[1][:], ALU.mult)
                if ti == NTILE - 1:
                    P.dma("sp", FF.v(FF.t[gi], gi), row[:])

            groups = [[fw.v(fw.t[i * 88 + j]), fw.v(fw.t[i * 88 + 44 + j])] for j in range(FFC)]
            gemm_fm(H, 16, TILES5, groups, WS, epi, psr=(0, 7))
        P.barrier()

    def setup_small():
        P.scope = "small"
        with ExitStack() as ph:
            A1 = P.sb(ph, "a1", [128, 16], F32)
            A2 = P.sb(ph, "a2", [128, 16], F32)
            for j in range(2):
                lam = pfc("llam%d" % j, 0, 16)
                P.stt("dve", A1[:], lam, -1.0, lam, ALU.mult, ALU.max)
                P.act(A1[:], A1[:], AF.Exp, scale=-1.0)
                P.act(A1[:], A1[:], AF.Ln, bias=1.0, scale=1.0)
                P.ts("dve", A2[:], lam, -1.0, 0.0, ALU.mult, ALU.max)
                P.tt("dve", A2[:], A2[:], A1[:], ALU.add)
                P.ts("dve", CNEG.v(CNEG.t[:, j, :]), A2[:], -8.0, None, ALU.mult)
            TMPL = P.sb(ph, "tmpl", [1, 64], F32)
            for j in range(2):
                lam_init = 0.8 - 0.6 * math.exp(-0.3 * (2 * j))
                for k in range(2):
                    o = j * 256 + k * 128
                    P.tt("dve", TMPL[:], DAL.v(DAL.t[:, o:o + 64]), DAL.v(DAL.t[:, o + 64:o + 128]), ALU.mult)
                    acc = DAL2.v(DAL2.t[:, j * 4 + k:j * 4 + k + 1])
                    P.memset("dve", acc, 0.0)
                    P.act(TMPL[:], TMPL[:], AF.Identity, accum_out=acc)
                    P.act(acc, acc, AF.Exp)
                res = DAL2.v(DAL2.t[:, j * 4 + 2:j * 4 + 3])
                P.tt("dve", res, DAL2.v(DAL2.t[:, j * 4 + 1:j * 4 + 2]), DAL2.v(DAL2.t[:, j * 4:j * 4 + 1]), ALU.subtract)
                P.ts("dve", res, res, -lam_init, None, ALU.add)
                ps = nextps()
                P.mm(ps.v(ps.t[:, 0:1]), CF.v(CF.t[0:1, C_ONE, :]), res)
                P.copy("dve", NLAM.v(NLAM.t[:, j:j + 1]), ps.v(ps.t[:, 0:1]))
        P.barrier()

    def conv4(dst, src, wcol, bcol):
        for si, (s0, s1) in enumerate(SEGS):
            eng = "dve"
            P.ts(eng, dst.v(dst.t[:, s0:s1], si), src.v(src.t[:, s0:s1]), wcol(2), bcol, ALU.mult, ALU.add)
            P.stt(eng, dst.v(dst.t[:, s0 + 2:s1], si), src.v(src.t[:, s0:s1 - 2]), wcol(0), dst.v(dst.t[:, s0 + 2:s1], si), ALU.mult, ALU.add)
            P.stt(eng, dst.v(dst.t[:, s0 + 1:s1], si), src.v(src.t[:, s0:s1 - 1]), wcol(1), dst.v(dst.t[:, s0 + 1:s1], si), ALU.mult, ALU.add)
            P.stt(eng, dst.v(dst.t[:, s0:s1 - 1], si), src.v(src.t[:, s0 + 1:s1]), wcol(3), dst.v(dst.t[:, s0:s1 - 1], si), ALU.mult, ALU.add)

    def lru_phase(j):
        P.scope = "L%d_lru" % cur_layer[0]
        with ExitStack() as ph:
            LW = P.sb(ph, "lw", [128, 2, 2, 8, 128], BF16, ntok=2)
            P.dma("pool", LW.v(LW.t[:, 0].rearrange("p d k j -> p (d k j)"), 0), IN["lwr"].v(IN["lwr"].t[j]))
            P.dma("pool", LW.v(LW.t[:, 1].rearrange("p d k j -> p (d k j)"), 1), IN["lwi"].v(IN["lwi"].t[j]))
            GA = [P.sb(ph, "ga", [128, T], F32) for _ in range(2)]
            XR = [P.sb(ph, "xr", [128, T], F32) for _ in range(2)]
            XCs = [P.sb(ph, "xc", [128, T], F32, ntok=3) for _ in range(2)]
            XCbs = [P.sb(ph, "xcb", [128, T], BF16) for _ in range(2)]
            RAs = [P.sb(ph, "ra", [128, T], F32) for _ in range(2)]
            GBs = [P.sb(ph, "gb", [128, T], F32) for _ in range(2)]
            HF = P.sb(ph, "hf", [128, T], F32)
            HB = P.sb(ph, "hb", [128, T], F32)
            REC = [P.sb(ph, "rec", [128, T], BF16) for _ in range(2)]
            FS = P.sb(ph, "fs", [128, 8, 2, 2], F32)

            def load(cc):
                P.dma("sp", GA[cc % 2][:], UFM.v(UFM.t[cc], cc))
                P.dma("sp", XR[cc % 2][:], UFM.v(UFM.t[8 + cc], 8 + cc))

            load(0)
            for cc in range(8):
                if cc + 1 < 8:
                    load(cc + 1)
                ga, xr = GA[cc % 2], XR[cc % 2]
                XC, XCb = XCs[cc % 2], XCbs[cc % 2]
                P.act(ga[:], ga[:], AF.Gelu_apprx_tanh)
                conv4(XC, xr, lambda k: pfc("lcw%d" % j, k * 8 + cc), pfc("lcb%d" % j, cc))
                P.copy("act", XCb[:], XC[:])
                for d in range(2):
                    HS = HF if d == 0 else HB
                    RA, GB = RAs[d], GBs[d]
                    for t in range(NTILE):
                        sl = slice(t * 512, (t + 1) * 512)
                        ps_r = nextps()
                        P.mm(ps_r[:], LW.v(LW.t[:, 0, d, cc, :], 0), XCb.v(XCb.t[:, sl]))
                        ps_i = nextps()
                        P.mm(ps_i[:], LW.v(LW.t[:, 1, d, cc, :], 1), XCb.v(XCb.t[:, sl]))
                        P.act(RA.v(RA.t[:, sl]), ps_r[:], AF.Sigmoid, bias=pfc("lbr%d" % j, d * 8 + cc))
                        P.act(GB.v(GB.t[:, sl]), ps_i[:], AF.Sigmoid, bias=pfc("lbi%d" % j, d * 8 + cc))
                    P.act(RA[:], RA[:], AF.Exp, scale=CNEG.v(CNEG.t[:, j, d * 8 + cc:d * 8 + cc + 1]))
                    P.tt("dve", GB[:], GB[:], XC[:], ALU.mult)
                    P.act(HS[:], RA[:], AF.Square)
                    P.act(HS[:], HS[:], AF.Sqrt, bias=1.0, scale=-1.0)
                    P.tt("dve", GB[:], GB[:], HS[:], ALU.mult)
                    for si, (s0, s1) in enumerate(SEGS):
                        init = pfc("lh0%d" % j, d * 8 + cc) if si == 0 else 0.0
                        if d == 0:
                            P.scan(HS.v(HS.t[:, s0:s1]), RA.v(RA.t[:, s0:s1]), GB.v(GB.t[:, s0:s1]), init)
                        else:
                            P.scan(HS.v(HS.t[:, s0:s1][:, ::-1]), RA.v(RA.t[:, s0:s1][:, ::-1]),
                                   GB.v(GB.t[:, s0:s1][:, ::-1]), init)
                for pi in range(2):
                    s0, s1 = SEGS[1 + pi]
                    P.copy("dve", FS.v(FS.t[:, cc, pi, 0:1]), HF.v(HF.t[:, s1 - 1:s1]))
                    P.copy("dve", FS.v(FS.t[:, cc, pi, 1:2]), HB.v(HB.t[:, s0:s0 + 1]))
                rec = REC[cc % 2]
                P.tt("dve", HF[:], HF[:], HB[:], ALU.add)
                P.tt("dve", rec[:], HF[:], ga[:], ALU.mult)
                P.dma("sp", MIX.v(MIX.t[cc], cc), rec[:])
            P.dma("sp", OUT["olru"].v(OUT["olru"].t[j]), FS.v(FS.t[:].rearrange("p a b c -> p (a b c)")))
        P.barrier()

    def att_phase(j, i):
        P.scope = "L%d_att" % i
        lam_init = 0.8 - 0.6 * math.exp(-0.3 * i)
        with ExitStack() as ph:
            CS = P.sb(ph, "cs", [128, 2, S_LEN], F32)
            P.dma("sp", CS.v(CS.t[:].rearrange("p a t -> p (a t)")), IN["ropecs"][:])
            Q32 = [P.sb(ph, "q32", [128, T], F32) for _ in range(2)]
            K32 = [P.sb(ph, "k32", [128, T], F32) for _ in range(2)]
            V32 = [P.sb(ph, "v32", [128, NBLK, 128], F32, ntok=4) for _ in range(2)]
            QT = P.sb(ph, "qt", [128, T], BF16, ntok=5)
            KT = P.sb(ph, "kt", [128, T + 256], BF16, ntok=6)
            VE = P.sb(ph, "ve", [128, 22, 129], BF16, ntok=4)
            ATTb = [P.sb(ph, "attb", [128, T], BF16) for _ in range(2)]
            SQ = [P.sb(ph, "asq", [128, 512], BF16) for _ in range(2)]
            RT = [P.sb(ph, "art", [128, 512], F32) for _ in range(2)]
            XN = [P.sb(ph, "axn", [128, 512], F32) for _ in range(2)]
            T1 = [P.sb(ph, "at1", [128, 512], F32) for _ in range(2)]
            T2 = [P.sb(ph, "at2", [128, 512], F32) for _ in range(2)]
            PT = [P.sb(ph, "apt", [128, 512], BF16) for _ in range(4)]
            KO = P.sb(ph, "ako", [128, 4, 128], F32)
            P.memset("dve", VE.v(VE.t[:, :, 128:129], 3), 1.0)
            cnt = {"n": 0, "sc": 0, "pt": 0, "rz": 0, "e": 0}

            def load(hd):
                P.dma("sp", Q32[hd % 2][:], UFM.v(UFM.t[16 + hd], 16 + hd))
                P.dma("sp", K32[hd % 2][:], UFM.v(UFM.t[24 + hd], 24 + hd))
                v = V32[hd % 2]
                src = UTM.t[:, hd * 128:(hd + 1) * 128].rearrange("(b p) c -> p b c", p=128)
                for q in range(4):
                    P.dma("sp", v.v(v.t[:, q * 5:(q + 1) * 5, :], q), UTM.v(src[:, q * 5:(q + 1) * 5, :], hd // 4))

            def qknorm(src, dst, gname, t, is_k, hd):
                n = cnt["n"]
                cnt["n"] += 1
                sl = slice(t * 512, (t + 1) * 512)
                sq, rt, xn = SQ[n % 2], RT[n % 2], XN[n % 2]
                P.act(sq[:], src.v(src.t[:, sl]), AF.Square)
                ps = PS[7]
                P.mm(ps[:], B64B, sq[:])
                P.act(rt[:], ps[:], AF.Ln, bias=EPS, scale=1.0)
                P.act(rt[:], rt[:], AF.Exp, scale=-0.5)
                P.stt("dve", xn[:], src.v(src.t[:, sl]), pfc(gname, 0), rt[:], ALU.mult, ALU.mult)
                if t < 4:
                    ps2 = PS[7]
                    P.mm(ps2[:], cfm(C_ROT), xn[:])
                    t1, t2 = T1[n % 2], T2[n % 2]
                    P.tt("dve", t1[:], xn[:], CS.v(CS.t[:, 0, sl]), ALU.mult)
                    P.tt("dve", t2[:], ps2[:], CS.v(CS.t[:, 1, sl]), ALU.mult)
                    P.tt("dve", dst.v(dst.t[:, sl], t), t1[:], t2[:], ALU.add)
                else:
                    P.copy("act", dst.v(dst.t[:, sl], t), xn[:])
                    if is_k:
                        for blk in range(4):
                            ps3 = PS[7]
                            P.transpose(ps3.v(ps3.t[:, 0:128]), xn.v(xn.t[:, blk * 128:(blk + 1) * 128]), cfm(C_ID))
                            evac(KO.v(KO.t[:, blk, :]), ps3.v(ps3.t[:, 0:128]))
                        okv = OUT["ok"].t[j].rearrange("a (b p) c -> p (a b) c", p=128)[:, :, hd * 128:(hd + 1) * 128]
                        P.dma("sp", OUT["ok"].v(okv), KO[:])

            def attend(q0, nq, keys, att):
                qtok = q0 // 512
                nk = len(keys)
                e = cnt["e"]
                cnt["e"] += 1
                par = e % 2
                psO = [PS[4], PS[5]]
                zbuf = [[ZA[par * 4 + m * 2], ZA[par * 4 + m * 2 + 1]] for m in range(2)]
                zacc = [None, None]

                def score(ki):
                    kc0, ktok, vb, vtok = keys[ki]
                    banks = [PS[(cnt["sc"] % 2) * 2], PS[(cnt["sc"] % 2) * 2 + 1]]
                    cnt["sc"] += 1
                    for m in range(2):
                        P.mm(banks[m].v(banks[m].t[:, 0:nq]), KT.v(KT.t[m * 64:(m + 1) * 64, kc0:kc0 + 128], ktok),
                             QT.v(QT.t[m * 64:(m + 1) * 64, q0:q0 + nq], qtok))
                    return banks

                pend = [score(0)]
                for ki in range(nk):
                    kc0, ktok, vb, vtok = keys[ki]
                    banks = pend.pop(0)
                    pts = []
                    for m in range(2):
                        pt = PT[cnt["pt"] % 4]
                        cnt["pt"] += 1
                        P.act(pt.v(pt.t[:, 0:nq]), banks[m].v(banks[m].t[:, 0:nq]), AF.Exp, scale=0.125)
                        pts.append(pt)
                    if ki + 1 < nk:
                        pend.append(score(ki + 1))
                    if ki == min(1, nk - 1) and defer:
                        for f in defer:
                            f()
                        del defer[:]
                    for m in range(2):
                        P.mm(psO[m].v(psO[m].t[:, 0:nq]), VE.v(VE.t[:, vb, 0:128], vtok), pts[m].v(pts[m].t[:, 0:nq]),
                             start=(ki == 0), stop=(ki == nk - 1))
                    for m, eng in ((0, "dve"), (1, EW2)):
                        z = zbuf[m][ki % 2]
                        if ki == 0:
                            P.copy(eng, z.v(z.t[:, 0:nq]), pts[m].v(pts[m].t[:, 0:nq]))
                        else:
                            zo = zbuf[m][(ki - 1) % 2]
                            P.tt(eng, z.v(z.t[:, 0:nq]), zo.v(zo.t[:, 0:nq]), pts[m].v(pts[m].t[:, 0:nq]), ALU.add)
                        zacc[m] = z
                t0, t1 = TT[par * 2], TT[par * 2 + 1]
                for m, t in enumerate((t0, t1)):
                    P.copy("dve", t.v(t.t[:, 0:nq]), psO[m].v(psO[m].t[:, 0:nq]))

                def epilogue():
                    r = [RR[par * 2], RR[par * 2 + 1]]
                    for m in range(2):
                        psz = PS[6]
                        P.mm(psz.v(psz.t[:, 0:nq]), cfm(C_ONE), zacc[m].v(zacc[m].t[:, 0:nq]))
                        P.act(r[m].v(r[m].t[:, 0:nq]), psz.v(psz.t[:, 0:nq]), AF.Ln)
                        P.act(r[m].v(r[m].t[:, 0:nq]), r[m].v(r[m].t[:, 0:nq]), AF.Exp, scale=-1.0)
                    P.tt("dve", t0.v(t0.t[:, 0:nq]), t0.v(t0.t[:, 0:nq]), r[0].v(r[0].t[:, 0:nq]), ALU.mult)
                    P.tt(EW2, t1.v(t1.t[:, 0:nq]), t1.v(t1.t[:, 0:nq]), r[1].v(r[1].t[:, 0:nq]), ALU.mult)
                    a_, sq = r[0], SQ2[par]
                    P.stt("dve", a_.v(a_.t[:, 0:nq]), t1.v(t1.t[:, 0:nq]), NLAM.v(NLAM.t[:, j:j + 1]), t0.v(t0.t[:, 0:nq]),
                          ALU.mult, ALU.add)
                    P.act(sq.v(sq.t[:, 0:nq]), a_.v(a_.t[:, 0:nq]), AF.Square)
                    ps = PS[7]
                    P.mm(ps.v(ps.t[:, 0:nq]), ONESB, sq.v(sq.t[:, 0:nq]))
                    rt = r[1]
                    P.act(rt.v(rt.t[:, 0:nq]), ps.v(ps.t[:, 0:nq]), AF.Ln, bias=EPS, scale=1.0 / 128)
                    P.act(rt.v(rt.t[:, 0:nq]), rt.v(rt.t[:, 0:nq]), AF.Exp, scale=-0.5)
                    P.stt("dve", att.v(att.t[:, q0:q0 + nq]), a_.v(a_.t[:, 0:nq]), SUBG[:], rt.v(rt.t[:, 0:nq]), ALU.mult, ALU.mult)

                defer.append(epilogue)

            defer = []
            RR = [P.sb(ph, "arr", [128, 512], F32) for _ in range(4)]
            TT = [P.sb(ph, "att_t", [128, 512], F32) for _ in range(4)]
            ZA = [P.sb(ph, "aza", [128, 512], F32) for _ in range(8)]
            SQ2 = [P.sb(ph, "asq2", [128, 512], BF16) for _ in range(2)]
            SUBG = P.sb(ph, "subg", [128, 1], F32)
            P.ts("dve", SUBG[:], pfc("sub%d" % j, 0), 1.0 - lam_init, None, ALU.mult)

            load(0)
            for hd in range(8):
                if hd + 1 < 8:
                    load(hd + 1)
                q32, k32, v32 = Q32[hd % 2], K32[hd % 2], V32[hd % 2]
                att = ATTb[hd % 2]
                P.dma("pool", KT.v(KT.t[:, T:T + 256], 5), IN["kcT"].v(IN["kcT"].t[j * 8 + hd]))
                vcv = IN["vc"].t[j, :, hd * 128:(hd + 1) * 128].rearrange("(b p) c -> p b c", p=128)
                P.dma("pool", VE.v(VE.t[:, 20:22, 0:128], 2), IN["vc"].v(vcv))
                for t in range(NTILE):
                    qknorm(q32, QT, "qn%d" % j, t, False, hd)
                    qknorm(k32, KT, "kn%d" % j, t, True, hd)
                P.copy("act", VE.v(VE.t[:, 0:16, 0:128], 0), v32.v(v32.t[:, 0:16, :]))
                P.copy("dve", VE.v(VE.t[:, 16:20, 0:128], 1), v32.v(v32.t[:, 16:20, :], 3))
                ovv = OUT["ov"].t[j].rearrange("a (b p) c -> p (a b) c", p=128)[:, :, hd * 128:(hd + 1) * 128]
                P.dma("sp", OUT["ov"].v(ovv), v32.v(v32.t[:, 16:20, :], 3))
                skeys = [(kb * 128, kb // 4, kb, 0) for kb in range(16)] + [(T, 5, 20, 2), (T + 128, 5, 21, 2)]
                for qt in range(4):
                    attend(qt * 512, 512, skeys, att)
                for pi in range(2):
                    q0 = S_LEN + pi * P_LEN
                    attend(q0, 256, [(q0, 4, 16 + 2 * pi, 1), (q0 + 128, 4, 17 + 2 * pi, 1)], att)
                for f in defer:
                    f()
                del defer[:]
                P.dma("sp", MIX.v(MIX.t[8 + hd], 8 + hd), att[:])
        P.barrier()

    SEQ_CHUNKS = [list(range(0, 16)), [16, 17], [18, 19]]
    EW2 = os.environ.get("EW2", "pool")
    ssd_dbg = stop_after[1] if (stop_after is not None and stop_after[0] == "ssdonly") else None

    def ssd_phase(j):
        P.scope = "L%d_ssd" % cur_layer[0]
        with ExitStack() as ph:
            DTA = P.sb(ph, "dta", [128, NBLK, 128], F32)
            AALL = P.sb(ph, "aall", [128, NBLK, 128], F32)
            AH = P.sb(ph, "ah", [128, NBLK, 128], BF16)
            AL = P.sb(ph, "al", [128, NBLK, 128], BF16)
            with ExitStack() as p2:
                XX = P.sb(p2, "sx", [128, NBLK, 128], F32)
                EE = P.sb(p2, "se", [128, NBLK, 128], F32)
                AR = P.sb(p2, "sar", [128, 128], F32)
                src = UTM.t[:, 4096:4224].rearrange("(b p) c -> p b c", p=128)
                for q in range(4):
                    P.dma("sp", DTA.v(DTA.t[:, q * 5:(q + 1) * 5, :]), UTM.v(src[:, q * 5:(q + 1) * 5, :], 8))
                dtb = prc("dtb%d" % j, 0, 128)
                P.tt("dve", XX[:], DTA[:], View(bc(dtb.ap, 1, [128, NBLK, 128]), dtb.toks), ALU.add)
                P.stt("dve", EE[:], XX[:], -1.0, XX[:], ALU.mult, ALU.max)
                P.act(EE[:], EE[:], AF.Exp, scale=-1.0)
                P.act(EE[:], EE[:], AF.Ln, bias=1.0, scale=1.0)
                P.stt("dve", DTA[:], XX[:], 0.0, EE[:], ALU.max, ALU.add)
                P.act(AR[:], prc("alog%d" % j, 0, 128), AF.Exp)
                P.stt("dve", AALL[:], DTA[:], -1.0, View(bc(AR.t[:], 1, [128, NBLK, 128]), AR.toks), ALU.mult, ALU.mult)
                P.copy("dve", AH[:], AALL[:])
                P.tt("dve", XX[:], AALL[:], AH[:], ALU.subtract)
                P.copy("dve", AL[:], XX[:])
            P.barrier()
            if ssd_dbg is not None and ssd_dbg == 0:
                return
            for g in range(8 if (ssd_dbg is None or ssd_dbg >= 4) else 1):
                ssd_group(j, g, DTA, AALL, AH, AL)
        P.barrier()

    def ssd_group(j, g, DTA, AALL, AH, AL):
        with ExitStack() as ph:
            BTf = P.sb(ph, "btf", [128, T], BF16)
            CTf = P.sb(ph, "ctf", [128, T], BF16)
            BTM = P.sb(ph, "btm", [128, NBLK, 128], BF16)
            XTM = P.sb(ph, "xtm", [128, NBLK, 512], BF16)
            SENTS = P.sb(ph, "sents", [128, NBLK, 512], BF16, ntok=NBLK)
            YGT = P.sb(ph, "ygt", [128, 4, T], BF16)
            EL = P.sb(ph, "el", [128, 2, NBLK, 8], F32)
            DEC = P.sb(ph, "dec", [128, 2, NBLK, 8], F32)
            ETOT = P.sb(ph, "etot", [128, 2, NBLK, 8], F32)
            S0 = P.sb(ph, "s0", [128, 2, 512], F32)
            NW = P.sb(ph, "nw", [128, 512], F32)
            P.dma("sp", NW[:], IN["normw"].v(IN["normw"].t[j, :, g * 512:(g + 1) * 512]))
            for d in range(2):
                P.dma("sp", S0.v(S0.t[:, d, :]), IN["ssd0T"].v(IN["ssd0T"].t[j * 2 + d, :, g * 512:(g + 1) * 512]))
            with ExitStack() as p2:
                RAW = [P.sb(p2, "raw", [128, T], F32) for _ in range(2)]
                CVs = [P.sb(p2, "cv", [128, T], F32, ntok=3) for _ in range(2)]
                XSbs = [P.sb(p2, "xsb", [128, T], BF16) for _ in range(2)]
                ACS = P.sb(p2, "acs", [128, 2, 2, NBLK, 8], F32)
                TD = P.sb(p2, "td", [128, 2, NBLK, 8], F32)
                chunks = [32 + g, 40 + g] + [g * 4 + i for i in range(4)]

                def load(k):
                    P.dma("sp", RAW[k % 2][:], UFM.v(UFM.t[chunks[k]], chunks[k]))

                load(0)
                for k, ch in enumerate(chunks):
                    if k + 1 < len(chunks):
                        load(k + 1)
                    CV = CVs[k % 2]
                    conv4(CV, RAW[k % 2], lambda tap, ch=ch: pfc("scw%d" % j, tap * 48 + ch), pfc("scb%d" % j, ch))
                    dst = BTf if k == 0 else (CTf if k == 1 else XSbs[k % 2])
                    P.act(dst[:], CV[:], AF.Silu)
                    if k == 1:
                        continue
                    for c4 in range(NBLK // 4):
                        pst = PS[7]
                        pb = pst.t[:].bitcast(BF16)
                        for cc in range(4):
                            c = c4 * 4 + cc
                            P.transpose(pst.v(pb[:, cc * 128:(cc + 1) * 128]), dst.v(dst.t[:, c * 128:(c + 1) * 128]), IDB)
                        srcv = pst.v(pb[:, 0:512].rearrange("p (a b) -> p a b", a=4))
                        if k == 0:
                            evac(BTM.v(BTM.t[:, c4 * 4:(c4 + 1) * 4, :]), srcv)
                        else:
                            i = k - 2
                            evac(XTM.v(XTM.t[:, c4 * 4:(c4 + 1) * 4, i * 128:(i + 1) * 128]), srcv)
                for d in range(2):
                    ps = PS[d]
                    c0 = d * 64 + g * 8
                    for half, lhs in enumerate((CB.v(CB.t[:, 5 + d, :]), ONESB)):
                        for x, AX in enumerate((AH, AL)):
                            P.mm(ps.v(ps.t[:, half * 160:(half + 1) * 160].rearrange("p (c k) -> p c k", k=8)), lhs,
                                 AX.v(AX.t[:, :, c0:c0 + 8]), start=(x == 0), stop=(x == 1))
                    P.copy("act", ACS.v(ACS.t[:, d].rearrange("p a c k -> p (a c k)")), ps.v(ps.t[:, 0:320]))
                    P.act(EL.v(EL.t[:, d]), ACS.v(ACS.t[:, d, 0]), AF.Exp)
                    P.act(ETOT.v(ETOT.t[:, d]), ACS.v(ACS.t[:, d, 1]), AF.Exp)
                    P.tt("dve", TD.v(TD.t[:, d]), ACS.v(ACS.t[:, d, 1]), ACS.v(ACS.t[:, d, 0]), ALU.subtract)
                    P.act(DEC.v(DEC.t[:, d]), TD.v(TD.t[:, d]), AF.Exp)
            P.barrier()
            if ssd_dbg is not None and ssd_dbg == 1:
                return
            with ExitStack() as p3:
                SENT = P.sb(p3, "sent", [128, 512], F32)
                SENTb = P.sb(p3, "sentb", [128, 512], BF16)
                XTd = [P.sb(p3, "xtd", [128, 512], BF16) for _ in range(4)]
                XD = [P.sb(p3, "xd", [128, 512], BF16) for _ in range(2)]
                CBM = [P.sb(p3, "cbm", [128, 2, 128], F32) for _ in range(2)]
                AU = [P.sb(p3, "au", [128, 2, 8, 128], BF16) for _ in range(4)]
                EXPD = [P.sb(p3, "expd", [128, 8, 128], F32) for _ in range(4)]
                MT = [P.sb(p3, "mt", [128, 8, 128], BF16) for _ in range(4)]
                YO = [P.sb(p3, "yo", [128, 512], F32) for _ in range(2)]
                YY = [P.sb(p3, "yy", [128, 512], F32) for _ in range(2)]
                XS2 = [P.sb(p3, "xs2", [128, 512], F32) for _ in range(2)]
                ZT = [P.sb(p3, "zt", [128, 512], F32) for _ in range(2)]
                JK = P.sb(p3, "sjk", [128, 512], F32)
                YN = [P.sb(p3, "yn", [128, 512], BF16) for _ in range(2)]
                SS = [P.sb(p3, "sss", [128, 1], F32) for _ in range(2)]

                def h8(tile_view_ap):
                    return tile_view_ap.rearrange("p (h q) -> p h q", h=8)

                def hb8(tl, d, c):
                    return View(tl.t[:, d, c, :].unsqueeze(2).broadcast_to([128, 8, 64]), tl.toks)

                def dt8(c, d):
                    ap = DTA.t[:, c, d * 64 + g * 8:d * 64 + g * 8 + 8]
                    return View(ap.unsqueeze(2).broadcast_to([128, 8, 64]), DTA.toks)

                def make_xd(c, d, xt):
                    xd = XD[c % 2]
                    P.tt(EW2, View(h8(xd.t[:]), xd.toks), View(h8(xt.t[:]), xt.toks), hb8(DEC, d, c), ALU.mult)
                    return xd

                def state_update(S, c, d, xd):
                    psS = PS[6]
                    P.mm(psS[:], BTM.v(BTM.t[:, c, :]), xd[:])
                    P.tt("dve", View(h8(S.t[:]), S.toks), View(h8(S.t[:]), S.toks), hb8(ETOT, d, c), ALU.mult)
                    P.tt("dve", S[:], S[:], psS[:], ALU.add)

                for si, chs in enumerate(SEQ_CHUNKS):
                    if si == 0:
                        P.copy("dve", SENT[:], S0.v(S0.t[:, 0, :]))
                    else:
                        P.memset("dve", SENT[:], 0.0)
                    for c in chs:
                        P.copy("act", SENTS.v(SENTS.t[:, c, :], c), SENT[:])
                        xt = XTd[c % 2]
                        P.tt(EW2, View(h8(xt.t[:]), xt.toks), XTM.v(h8(XTM.t[:, c, :])), dt8(c, 0), ALU.mult)
                        state_update(SENT, c, 0, make_xd(c, 0, xt))
                    if si > 0:
                        P.dma("sp", OUT["ossd"].v(OUT["ossd"].t[(j * 2 + si - 1) * 2 + 0, :, g * 512:(g + 1) * 512]), SENT[:])
                if ssd_dbg is not None and ssd_dbg == 2:
                    return
                dsk = prc("dsk%d" % j, g * 8, g * 8 + 8)
                dskb = View(dsk.ap.unsqueeze(2).broadcast_to([128, 8, 64]), dsk.toks)
                order = [(si, c) for si, chs in enumerate(SEQ_CHUNKS) for c in reversed(chs)]

                def loadz(k):
                    c = order[k][1]
                    P.dma("sp", ZT[k % 2][:], UTM.v(UTM.t[c * 128:(c + 1) * 128, g * 512:(g + 1) * 512], g))

                def common(k):
                    c = order[k][1]
                    par = k % 2
                    st = {"c": c, "par": par, "si": order[k][0], "xts": [None, None], "mts": [None, None]}
                    cbm = CBM[par]
                    psC = PS[7]
                    pcv = psC.v(psC.t[:, 256:384])
                    P.mm(pcv, BTf.v(BTf.t[:, c * 128:(c + 1) * 128]), CTf.v(CTf.t[:, c * 128:(c + 1) * 128]))
                    P.tt("dve", cbm.v(cbm.t[:, 0, :]), pcv, cfm(C_MF), ALU.mult)
                    P.tt("dve", cbm.v(cbm.t[:, 1, :]), pcv, cfm(C_MB), ALU.mult)
                    xs2 = XS2[par]
                    P.tt(EW2, View(h8(xs2.t[:]), xs2.toks), XTM.v(h8(XTM.t[:, c, :])), dskb, ALU.mult)
                    st["xs2"] = xs2
                    return st

                def indep_d(st, d):
                    c, par = st["c"], st["par"]
                    cbm = CBM[par]
                    au, expd, mt = AU[2 * par + d], EXPD[2 * par + d], MT[2 * par + d]
                    mask = cfm(C_MF if d == 0 else C_MB)
                    maskb = View(mask.ap.unsqueeze(1).broadcast_to([128, 8, 128]), mask.toks)
                    for x, AX in enumerate((AH, AL)):
                        arow = AX.t[:, c, d * 64 + g * 8:d * 64 + g * 8 + 8]
                        P.tt(EW2, au.v(au.t[:, x]), maskb, View(arow.unsqueeze(2).broadcast_to([128, 8, 128]), AX.toks), ALU.mult)
                    tri = CB.v(CB.t[:, 3 if d == 0 else 4, :])
                    for hh in range(2):
                        psD = PS[2 * d + hh]
                        for x in range(2):
                            P.mm(psD[:], tri, au.v(au.t[:, x, hh * 4:(hh + 1) * 4, :].rearrange("p a b -> p (a b)")),
                                 start=(x == 0), stop=(x == 1))
                        P.act(expd.v(expd.t[:, hh * 4:(hh + 1) * 4, :].rearrange("p a b -> p (a b)")), psD[:], AF.Exp)
                    P.tt("dve", mt[:], expd[:], View(cbm.t[:, d, :].unsqueeze(1).broadcast_to([128, 8, 128]), cbm.toks), ALU.mult)
                    xt = XTd[2 * par + d]
                    P.tt(EW2, View(h8(xt.t[:]), xt.toks), XTM.v(h8(XTM.t[:, c, :])), dt8(c, d), ALU.mult)
                    st["xts"][d] = xt
                    st["mts"][d] = mt
                    if d == 1:
                        st["xd"] = make_xd(c, 1, xt)

                def ymm(st):
                    psY = PS[4]
                    for d in range(2):
                        mt, xt = st["mts"][d], st["xts"][d]
                        for h in range(8):
                            P.mm(psY.v(psY.t[:, h * 64:(h + 1) * 64]), mt.v(mt.t[:, h, :]), xt.v(xt.t[:, h * 64:(h + 1) * 64]),
                                 start=(d == 0 and h == 0), stop=(d == 1), sgc=True)

                def recur_a(k, st):
                    si, c = order[k]
                    first = (k == 0 or order[k - 1][0] != si)
                    last = (k == len(order) - 1 or order[k + 1][0] != si)
                    if first:
                        if si == 0:
                            P.copy("dve", SENT[:], S0.v(S0.t[:, 1, :]))
                        else:
                            P.memset("dve", SENT[:], 0.0)
                    P.copy("act", SENTb[:], SENT[:])
                    for d in range(2):
                        psYo = PS[5]
                        sent_in = SENTS.v(SENTS.t[:, c, :], c) if d == 0 else SENTb[:]
                        P.mm(psYo[:], CTf.v(CTf.t[:, c * 128:(c + 1) * 128]), sent_in)
                        yo = YO[d]
                        P.tt("dve", View(h8(yo.t[:]), yo.toks), View(h8(psYo.t[:]), psYo.toks), hb8(EL, d, c), ALU.mult)
                    state_update(SENT, c, 1, st["xd"])
                    if last and si > 0:
                        P.dma("sp", OUT["ossd"].v(OUT["ossd"].t[(j * 2 + si - 1) * 2 + 1, :, g * 512:(g + 1) * 512]), SENT[:])

                def recur_b1(k, st):
                    yy = YY[st["par"]]
                    P.tt("dve", yy[:], YO[0][:], YO[1][:], ALU.add)
                    P.tt("dve", yy[:], yy[:], PS[4][:], ALU.add)

                def recur_b2(k, st):
                    c, par = st["c"], st["par"]
                    yy, zt, xs2 = YY[par], ZT[par], st["xs2"]
                    P.tt("dve", yy[:], yy[:], xs2[:], ALU.add)
                    P.act(zt[:], zt[:], AF.Silu)
                    P.tt(EW2, yy[:], yy[:], zt[:], ALU.mult)
                    ss = SS[par]
                    P.memset("dve", ss[:], 0.0)
                    P.act(JK[:], yy[:], AF.Square, accum_out=ss[:])
                    P.act(ss[:], ss[:], AF.Ln, bias=EPS, scale=1.0 / 512)
                    P.act(ss[:], ss[:], AF.Exp, scale=-0.5)
                    yn = YN[par]
                    P.stt("dve", yn[:], yy[:], ss[:], NW[:], ALU.mult, ALU.mult)
                    pst = PS[7]
                    pb = pst.t[:].bitcast(BF16)
                    for i4 in range(4):
                        P.transpose(pst.v(pb[:, i4 * 128:(i4 + 1) * 128]), yn.v(yn.t[:, i4 * 128:(i4 + 1) * 128]), IDB)
                    evac(YGT.v(YGT.t[:, :, c * 128:(c + 1) * 128]), pst.v(pb[:, 0:512].rearrange("p (a b) -> p a b", a=4)))

                loadz(0)
                nst = common(0)
                indep_d(nst, 0)
                indep_d(nst, 1)
                ymm(nst)
                for k in range(len(order)):
                    st = nst
                    more = k + 1 < len(order)
                    if more:
                        loadz(k + 1)
                        nst = common(k + 1)
                        indep_d(nst, 0)
                    recur_a(k, st)
                    if more:
                        indep_d(nst, 1)
                    recur_b1(k, st)
                    if more:
                        ymm(nst)
                    recur_b2(k, st)
                P.dma("sp", MIX.v(MIX.t[g * 4:(g + 1) * 4].rearrange("c p t -> p c t"), [g * 4 + q for q in range(4)]), YGT[:])
        P.barrier()

    if ssd_dbg is not None:
        ssd_phase(0)
        P.assemble()
        es.close()
        return nc
    ada_phase()
    setup_small()
    res_in = IN["xT"]
    res_in_tok = Tile(IN["xT"].t, 32)
    cur = res_in_tok
    for i in range(n_layers):
        cur_layer[0] = i
        j = i // 2
        last = (i == n_layers - 1)
        hs = ExitStack()
        H = P.sb(hs, "H", [128, 16, T], BF16, ntok=NTILE)
        norm_phase(cur, i, 0, H)
        if i % 2 == 0:
            ef, ev = IN["ewin_fm"], IN["ewin_v"]
            proj_phase(H, [ef.v(ef.t[j * 32 + k]) for k in range(32)], 0,
                       [(ev.v(ev.t[j * 2 + k]), 512, k * 512, k) for k in range(2)])
            hs.close()
            if stop_after == ("proj", i):
                break
            lru_phase(j)
            att_phase(j, i)
            if stop_after == ("mix", i):
                break
            out_phase(16, 16, (MIX, IN["ewout"], j * 16), cur, XRES, i, 2)
        else:
            sf, sz, sd = IN["swin_fm"], IN["swin_z"], IN["swin_dt"]
            proj_phase(H, [sf.v(sf.t[j * 48 + k]) for k in range(48)], 0,
                       [(sz.v(sz.t[j * 8 + k]), 512, k * 512, k) for k in range(8)] + [(sd.v(sd.t[j]), 128, 4096, 8)])
            hs.close()
            if stop_after == ("proj", i):
                break
            ssd_phase(j)
            if stop_after == ("mix", i):
                break
            out_phase(32, 32, (MIX, IN["swout"], j * 16), cur, XRES, i, 2)
        cur = XRES
        if stop_after == ("mixout", i):
            break
        hs = ExitStack()
        H = P.sb(hs, "H", [128, 16, T], BF16, ntok=NTILE)
        norm_phase(XRES, i, 1, H)
        ffn_in_phase(H, i)
        hs.close()
        dst = Tile(OUT["yT"].t, 32) if last else XRES
        out_phase(FFC, FFC, (FF, IN["fwout"], i * 16), XRES, dst, i, 5)
    P.assemble()
    es.close()
    return nc


_CACHE = {}


def run_cores(inp, cores, n_layers=DEPTH, debug=False, stop_after=None, trace=False):
    key = (n_layers, debug, stop_after)
    if key not in _CACHE:
        _CACHE[key] = build_program(n_layers, debug, stop_after)
    nc = _CACHE[key]
    sh = shared_inputs(inp)
    in_maps = []
    for b in cores:
        m = dict(sh)
        m.update(core_inputs(inp, b))
        in_maps.append(m)
    res = run_bass_kernel_spmd(nc, in_maps, core_ids=list(range(len(cores))), trace=trace)
    return res


def assemble_outputs(results):
    yp = np.zeros((16, P_LEN, D), np.float32)
    ys = np.zeros((8, S_LEN, D), np.float32)
    nk = np.zeros((16, 2, P_LEN, 8, 128), np.float32)
    nv = np.zeros((16, 2, P_LEN, 8, 128), np.float32)
    nl = np.zeros((16, 2, 2, D_RNN), np.float32)
    nss = np.zeros((16, 2, 2, 64, 64, 128), np.float32)
    for b, r in enumerate(results):
        y = np.asarray(r["yT"]).reshape(D, T).T
        ys[b] = y[:S_LEN]
        yp[2 * b] = y[S_LEN:S_LEN + P_LEN]
        yp[2 * b + 1] = y[S_LEN + P_LEN:]
        ok = np.asarray(r["ok"]).reshape(2, 2, P_LEN, 8, 128)
        ov = np.asarray(r["ov"]).reshape(2, 2, P_LEN, 8, 128)
        ol = np.asarray(r["olru"]).reshape(2, 128, 8, 2, 2)
        os_ = np.asarray(r["ossd"]).reshape(2, 2, 2, 128, 64, 64)
        for pi in range(2):
            nk[2 * b + pi] = ok[:, pi]
            nv[2 * b + pi] = ov[:, pi]
            nl[2 * b + pi] = ol[:, :, :, pi, :].transpose(0, 3, 2, 1).reshape(2, 2, D_RNN)
            nss[2 * b + pi] = os_[:, pi].transpose(0, 1, 3, 4, 2)
    return yp, ys, nk, nv, nl, nss


def kernel(**inputs):
    res = run_cores(inputs, list(range(NCORES)))
    return assemble_outputs(res.results)
```
